# Optimizing a Trainium2 kernel written in Bass

```python
import jax, jax.numpy as jnp
from jax import lax
import numpy as np

D_MODEL = 4096
BATCH = 2
SEQ = 8192
DEPTH = 1

N_HEADS_A = 16
HEAD_DIM_A = 128
N_KV_A = 4
D_A = N_HEADS_A * HEAD_DIM_A
KV_DIM = N_KV_A * HEAD_DIM_A
N_HEADS_IDX = 16
HEAD_DIM_IDX = 128
D_IDX_Q = N_HEADS_IDX * HEAD_DIM_IDX
TOPK_MAX = 256
Q_BLOCK = 128
ROPE_THETA = 10000.0
HEAD_DIM_B = 64
D_B = 2048
N_HEADS_B = D_B // HEAD_DIM_B
LORA_DECAY = 96
LORA_A = 96
LORA_GATE = 256
GN_EPS = 64e-5
IN_SIZES = (D_A, KV_DIM, KV_DIM, D_IDX_Q, HEAD_DIM_IDX, N_HEADS_IDX, D_B, D_B, D_B)
D_IN = D_A + 2 * KV_DIM + D_IDX_Q + HEAD_DIM_IDX + N_HEADS_IDX + 3 * D_B
D_FF = ((8 * D_MODEL // 3 + 255) // 256) * 256
D_PLE = 256
RMS_EPS = 1e-6

kernel_name = "hybrid_dsa_rwkv7_gated_block"


def _rms(x):
    x32 = x.astype(jnp.float32)
    return (x32 * lax.rsqrt(jnp.mean(x32 * x32, axis=-1, keepdims=True) + RMS_EPS)).astype(x.dtype)


def rmsnorm(x, g):
    return _rms(x) * g


def rope(t, positions):
    d = t.shape[-1]
    inv_freq = ROPE_THETA ** (-jnp.arange(0, d, 2, dtype=jnp.float32) / d)
    ang = positions.astype(jnp.float32)[..., None] * inv_freq
    cos = jnp.cos(ang)[:, :, None, :].astype(t.dtype)
    sin = jnp.sin(ang)[:, :, None, :].astype(t.dtype)
    t1, t2 = t[..., : d // 2], t[..., d // 2:]
    return jnp.concatenate([t1 * cos - t2 * sin, t1 * sin + t2 * cos], axis=-1)


def token_shift(t):
    return jnp.pad(t, ((0, 0), (1, 0), (0, 0)))[:, :-1]


def split_in(proj):
    outs, start = [], 0
    for size in IN_SIZES:
        outs.append(proj[..., start:start + size])
        start += size
    return outs


def dsa_attention(q, k, v, q_idx, k_idx, w_idx):
    B, S = q.shape[0], q.shape[1]
    topk = min(TOPK_MAX, S // 4)
    nb = S // Q_BLOCK
    key_pos = jnp.arange(S)

    def to_blocks(t):
        return jnp.moveaxis(t.reshape((B, nb, Q_BLOCK) + t.shape[2:]), 1, 0)

    def one_block(args):
        blk, q_blk, qi_blk, wi_blk = args
        q_pos = blk * Q_BLOCK + jnp.arange(Q_BLOCK)
        causal = key_pos[None, :] <= q_pos[:, None]
        logits = jnp.einsum('bqhd,bsd->bqhs', qi_blk, k_idx) * (HEAD_DIM_IDX ** -0.5)
        score = jnp.einsum('bqh,bqhs->bqs', wi_blk, jax.nn.relu(logits)).astype(jnp.float32)
        score = jnp.where(causal[None], score, -jnp.inf)
        _, idx = lax.top_k(score, topk)
        k_sel = jax.vmap(lambda kk, ii: kk[ii])(k, idx)
        v_sel = jax.vmap(lambda vv, ii: vv[ii])(v, idx)
        qg = q_blk.reshape(B, Q_BLOCK, N_KV_A, N_HEADS_A // N_KV_A, HEAD_DIM_A)
        s = jnp.einsum('bqgnd,bqkgd->bqgnk', qg, k_sel).astype(jnp.float32) * (HEAD_DIM_A ** -0.5)
        valid = idx <= q_pos[None, :, None]
        s = jnp.where(valid[:, :, None, None, :], s, -jnp.inf)
        prob = jax.nn.softmax(s, axis=-1).astype(v.dtype)
        o = jnp.einsum('bqgnk,bqkgd->bqgnd', prob, v_sel)
        return o.reshape(B, Q_BLOCK, D_A)

    out = lax.map(one_block, (jnp.arange(nb), to_blocks(q), to_blocks(q_idx), to_blocks(w_idx)))
    return jnp.moveaxis(out, 0, 1).reshape(B, S, D_A)


def rwkv7_time_mix(xn, r, k, v, mu_rkv, mu_wag, w0, w1, w2, a0, a1, a2, g1, g2,
                   k_k, k_a, r_k, ln_w, ln_b):
    B, S, _ = xn.shape
    dt = xn.dtype
    r = r + (token_shift(r) - r) * mu_rkv[0]
    k = k + (token_shift(k) - k) * mu_rkv[1]
    v = v + (token_shift(v) - v) * mu_rkv[2]
    xx = token_shift(xn) - xn
    xw = xn + xx * mu_wag[0]
    xa = xn + xx * mu_wag[1]
    xg = xn + xx * mu_wag[2]
    w = -jax.nn.softplus(-(w0 + jnp.tanh(xw @ w1) @ w2)) - 0.5
    decay = jnp.exp(-jnp.exp(w.astype(jnp.float32)))
    a = jax.nn.sigmoid(a0 + (xa @ a1) @ a2)
    g = jax.nn.sigmoid(xg @ g1) @ g2
    kk = (k * k_k).astype(jnp.float32).reshape(B, S, N_HEADS_B, HEAD_DIM_B)
    kk = kk / jnp.maximum(jnp.linalg.norm(kk, axis=-1, keepdims=True), 1e-12)
    k = k * (1.0 + (a - 1.0) * k_a)

    def heads(t):
        return jnp.moveaxis(t.astype(jnp.float32).reshape(B, S, N_HEADS_B, HEAD_DIM_B), 1, 0)

    a_h = a.astype(jnp.float32).reshape(B, S, N_HEADS_B, HEAD_DIM_B)
    xs = (heads(r), heads(decay), heads(k), heads(v),
          jnp.moveaxis(-kk, 1, 0), jnp.moveaxis(kk * a_h, 1, 0))

    def step(state, inp):
        r_t, w_t, k_t, v_t, a_t, b_t = inp
        sa = jnp.einsum('bhij,bhj->bhi', state, a_t)
        state = (state * w_t[:, :, None, :] + sa[..., None] * b_t[:, :, None, :]
                 + v_t[..., None] * k_t[:, :, None, :])
        return state, jnp.einsum('bhij,bhj->bhi', state, r_t)

    state0 = jnp.zeros((B, N_HEADS_B, HEAD_DIM_B, HEAD_DIM_B), jnp.float32)
    _, y = lax.scan(step, state0, xs)
    y = jnp.moveaxis(y, 0, 1)
    mu = jnp.mean(y, axis=-1, keepdims=True)
    var = jnp.mean(jnp.square(y - mu), axis=-1, keepdims=True)
    y = ((y - mu) * lax.rsqrt(var + GN_EPS)).reshape(B, S, D_B).astype(dt) * ln_w + ln_b
    rh = r.reshape(B, S, N_HEADS_B, HEAD_DIM_B)
    kh = k.reshape(B, S, N_HEADS_B, HEAD_DIM_B)
    vh = v.reshape(B, S, N_HEADS_B, HEAD_DIM_B)
    bonus = jnp.sum(rh * kh * r_k, axis=-1, keepdims=True) * vh
    return (y + bonus.reshape(B, S, D_B)) * g


def setup_inputs(seed: int = 0) -> dict:
    key = jax.random.key(seed)
    ks = jax.random.split(key, 40)
    L, D = DEPTH, D_MODEL

    def nrm(k, shape, scale):
        return jax.random.normal(k, shape, jnp.float32) * scale

    def uni(k, shape):
        return jax.random.uniform(k, shape, jnp.float32)

    return {
        "x": nrm(ks[0], (BATCH, SEQ, D), 1.0),
        "p": nrm(ks[1], (L, BATCH, SEQ, D_PLE), 1.0),
        "positions": jnp.broadcast_to(jnp.arange(SEQ, dtype=jnp.int32), (BATCH, SEQ)),
        "norm_mix": 1.0 + nrm(ks[2], (L, D), 0.02),
        "w_in": nrm(ks[3], (L, D, D_IN), D ** -0.5),
        "mu_rkv": uni(ks[4], (L, 3, D_B)),
        "mu_wag": uni(ks[5], (L, 3, D)),
        "w0": nrm(ks[6], (L, D_B), 0.5),
        "w1": nrm(ks[7], (L, D, LORA_DECAY), D ** -0.5),
        "w2": nrm(ks[8], (L, LORA_DECAY, D_B), 0.1 * LORA_DECAY ** -0.5),
        "a0": nrm(ks[9], (L, D_B), 0.1),
        "a1": nrm(ks[10], (L, D, LORA_A), D ** -0.5),
        "a2": nrm(ks[11], (L, LORA_A, D_B), 0.1 * LORA_A ** -0.5),
        "g1": nrm(ks[12], (L, D, LORA_GATE), D ** -0.5),
        "g2": nrm(ks[13], (L, LORA_GATE, D_B), LORA_GATE ** -0.5),
        "k_k": 0.85 + nrm(ks[14], (L, D_B), 0.02),
        "k_a": 1.0 + nrm(ks[15], (L, D_B), 0.02),
        "r_k": nrm(ks[16], (L, N_HEADS_B, HEAD_DIM_B), 0.05),
        "ln_w": 1.0 + nrm(ks[17], (L, D_B), 0.02),
        "ln_b": nrm(ks[18], (L, D_B), 0.02),
        "w_pa": nrm(ks[19], (L, D_A, D), D_A ** -0.5),
        "w_pb": nrm(ks[20], (L, D_B, D), D_B ** -0.5),
        "w_gate": nrm(ks[21], (L, D, 2 * D), D ** -0.5),
        "b_gate": nrm(ks[22], (L, 2 * D), 0.02),
        "w_o": nrm(ks[23], (L, D, D), D ** -0.5),
        "norm_ffn": 1.0 + nrm(ks[24], (L, D), 0.02),
        "w_ffn1": nrm(ks[25], (L, D, D_FF), D ** -0.5),
        "w_ffn3": nrm(ks[26], (L, D, D_FF), D ** -0.5),
        "w_ffn2": nrm(ks[27], (L, D_FF, D), D_FF ** -0.5),
        "w_ple_gate": nrm(ks[28], (L, D, D), D ** -0.5),
        "w_ple": nrm(ks[29], (L, D_PLE, D), D_PLE ** -0.5),
        "norm_final": 1.0 + nrm(ks[30], (D,), 0.02),
    }


def reference(x, p, positions, norm_mix, w_in, mu_rkv, mu_wag, w0, w1, w2, a0, a1, a2,
              g1, g2, k_k, k_a, r_k, ln_w, ln_b, w_pa, w_pb, w_gate, b_gate, w_o,
              norm_ffn, w_ffn1, w_ffn3, w_ffn2, w_ple_gate, w_ple, norm_final):
    B, S, _ = x.shape
    h = x
    for i in range(DEPTH):
        xn = rmsnorm(h, norm_mix[i])
        q, k, v, qi, ki, wi, rb, kb, vb = split_in(xn @ w_in[i])
        q = rope(q.reshape(B, S, N_HEADS_A, HEAD_DIM_A), positions)
        k = rope(k.reshape(B, S, N_KV_A, HEAD_DIM_A), positions)
        v = v.reshape(B, S, N_KV_A, HEAD_DIM_A)
        qi = rope(qi.reshape(B, S, N_HEADS_IDX, HEAD_DIM_IDX), positions)
        ki = rope(ki[:, :, None, :], positions)[:, :, 0, :]
        wi = wi * (N_HEADS_IDX ** -0.5)
        y_a = dsa_attention(q, k, v, qi, ki, wi) @ w_pa[i]
        y_b = rwkv7_time_mix(xn, rb, kb, vb, mu_rkv[i], mu_wag[i], w0[i], w1[i], w2[i],
                             a0[i], a1[i], a2[i], g1[i], g2[i], k_k[i], k_a[i], r_k[i],
                             ln_w[i], ln_b[i]) @ w_pb[i]
        gates = jax.nn.sigmoid(xn @ w_gate[i] + b_gate[i])
        g_a, g_b = gates[..., :D_MODEL], gates[..., D_MODEL:]
        h = h + (g_a * y_a + g_b * y_b) @ w_o[i]
        xf = rmsnorm(h, norm_ffn[i])
        h = h + (jax.nn.silu(xf @ w_ffn1[i]) * (xf @ w_ffn3[i])) @ w_ffn2[i]
        h = h + jax.nn.sigmoid(_rms(h) @ w_ple_gate[i]) * (p[i] @ w_ple[i])
    return rmsnorm(h, norm_final)
```

```python
import math
from contextlib import ExitStack

import numpy as np
import ml_dtypes

import concourse.bass as bass
import concourse.mybir as mybir
from concourse.bass_utils import run_bass_kernel_spmd

F32 = mybir.dt.float32
BF16 = mybir.dt.bfloat16
I32 = mybir.dt.int32
AF = mybir.ActivationFunctionType
ALU = mybir.AluOpType

FULL = dict(B=2, S=8192, D=4096, HA=16, G=4, HI=16, DB=2048, LW=96, LA=96, LG=256,
            DFF=11008, DPLE=256, T=512, TOPK=256, NBIS=22)


class Buf:
    __slots__ = ("name", "w", "r", "dsem", "multi")

    def __init__(self, name="", multi=False):
        self.name = name
        self.w = {}
        self.r = []
        self.dsem = None
        self.multi = multi


class Op:
    __slots__ = ("fn", "waits", "inc")

    def __init__(self, fn, waits, inc):
        self.fn = fn
        self.waits = waits
        self.inc = inc


ENGS = ("pe", "act", "dve", "pool", "sp")


class Sched:
    def __init__(self):
        self.ops = {e: [] for e in ENGS}
        self.cnt = {}
        self.seen = {e: {} for e in ENGS}
        self.floor = {e: None for e in ENGS}
        self.ndsem = 0
        self.nops = 0

    def op(self, eng, fn, reads=(), writes=(), dma=False, key=None, amt=None, nofloor=False):
        self.nops += 1
        deps = {}

        def flat(lst):
            out = []
            for b in lst:
                if isinstance(b, (tuple, list)):
                    out.extend(flat(b))
                else:
                    out.append(b)
            return out
        reads = flat(reads)
        writes = flat(writes)

        def add(k, v):
            if deps.get(k, 0) < v:
                deps[k] = v

        if dma:
            kb = key if key is not None else (writes[0] if writes else reads[0])
            if kb.dsem is None:
                kb.dsem = ("d", self.ndsem)
                self.ndsem += 1
            mykey = kb.dsem
        else:
            mykey = ("e", eng)
        own = 0
        for b in reads:
            for k, v in b.w.items():
                add(k, v)
                if k == mykey:
                    own = max(own, v)
        for b in writes:
            if not b.multi:
                for k, v in b.w.items():
                    if dma and k == mykey:
                        continue
                    add(k, v)
                    if k == mykey:
                        own = max(own, v)
            for (k, v) in b.r:
                add(k, v)
        if not dma and mykey in deps:
            if eng == "pe" or own == 0:
                del deps[mykey]
            else:
                deps[mykey] = own
        if self.floor[eng] is not None and not nofloor:
            for k, v in self.floor[eng].items():
                if (k != mykey or dma) and deps.get(k, 0) < v:
                    deps[k] = v
            self.floor[eng] = None
        seen = self.seen[eng]
        waits = []
        for k, v in deps.items():
            if seen.get(k, 0) < v:
                seen[k] = v
                waits.append((k, v))
        inc = amt if amt is not None else (16 if dma else 1)
        self.cnt[mykey] = self.cnt.get(mykey, 0) + inc
        tok = (mykey, self.cnt[mykey])
        for b in reads:
            b.r.append(tok)
            if len(b.r) > 64:
                m = {}
                for k, v in b.r:
                    if m.get(k, 0) < v:
                        m[k] = v
                b.r = list(m.items())
        for b in writes:
            if b.multi:
                if b.w.get(tok[0], 0) < tok[1]:
                    b.w[tok[0]] = tok[1]
            else:
                b.w = {tok[0]: tok[1]}
            b.r = []
        self.ops[eng].append(Op(fn, waits, (mykey, inc)))
        return tok

    def capture(self, fn):
        cap = []
        real = self.op
        self.op = lambda *a, **k: cap.append((a, k))
        try:
            fn()
        finally:
            self.op = real
        return cap

    def replay_interleaved(self, la, lb):
        ia = ib = 0
        na, nb = len(la), len(lb)
        while ia < na or ib < nb:
            if ib >= nb or (ia < na and ia * nb <= ib * na):
                a, k = la[ia]
                ia += 1
            else:
                a, k = lb[ib]
                ib += 1
            self.op(*a, **k)

    def barrier(self):
        snap = dict(self.cnt)
        for e in ENGS:
            self.floor[e] = dict(snap)

    def emit(self, nc, es, final_eng="sp"):
        self.barrier()
        fl = self.floor[final_eng]
        waits = [(k, v) for k, v in fl.items() if self.seen[final_eng].get(k, 0) < v]
        self.ops[final_eng].append(Op(None, waits, None))
        sems = {}
        for k in list(self.cnt.keys()):
            sems[k] = es.enter_context(nc.semaphore("s%s%s" % (k[0], k[1])))
        block = es.enter_context(nc.Block())
        engmap = {"pe": block.tensor, "act": block.scalar, "dve": block.vector,
                  "pool": block.gpsimd, "sp": block.sync}

        def make(ename):
            ops = self.ops[ename]

            def body(eng):
                for o in ops:
                    for k, v in o.waits:
                        eng.wait_ge(sems[k], v)
                    if o.fn is not None:
                        ins = o.fn(eng)
                        ins.then_inc(sems[o.inc[0]], o.inc[1])
            return body

        for ename in ENGS:
            engmap[ename](make(ename))


def derive(cfg):
    c = dict(cfg)
    c["KC"] = c["D"] // 128
    c["NT"] = c["S"] // c["T"]
    c["NOWN"] = c["NT"] // 4
    c["NTOK"] = c["NOWN"] * c["T"]
    c["NQB"] = c["NTOK"] // 128
    c["DBc"] = c["DB"] // 4
    c["NGB"] = c["DBc"] // 128
    c["NCH"] = c["T"] // 64
    c["FC"] = c["DFF"] // 128
    NP = 4
    base, rem = divmod(c["FC"], NP)
    c["PARTS"] = [base + (1 if i < rem else 0) for i in range(NP)]
    c["PMAX"] = max(c["PARTS"])
    c["HPG"] = c["HA"] // c["G"]
    c["DA"] = c["HA"] * 128
    c["YBC"] = c["DB"] // 128
    return c


def vec_layout(c):
    KC, NGB, NQB = c["KC"], c["NGB"], c["NQB"]
    items = [("nmix", KC), ("muw", KC), ("mua", KC), ("mug", KC), ("nffn", KC), ("nfin", KC),
             ("bga", KC), ("bgb", KC)]
    for n in ("mur", "muk", "muv", "w0", "a0", "kk", "ka", "rk", "lnw", "lnb"):
        items.append((n, NGB))
    items += [("invf", 1), ("sgn", 1), ("sel", 4), ("qpos", NQB)]
    lay = {}
    off = 0
    for n, w in items:
        lay[n] = (off, w)
        off += w
    return lay, off


CF_ITEMS = [("ident", 128), ("ones", 128), ("blk", 128), ("prot", 128), ("mk", 128), ("mkl", 64),
            ("caus", 128), ("i64", 64)]


def cf_layout():
    lay = {}
    off = 0
    for n, w in CF_ITEMS:
        lay[n] = (off, w)
        off += w
    return lay, off


def build(cfg, dbg=()):
    c = derive(cfg)
    S, D, KC, T, NT, NOWN, NTOK, NQB = c["S"], c["D"], c["KC"], c["T"], c["NT"], c["NOWN"], c["NTOK"], c["NQB"]
    HA, G, HI, HPG, DA = c["HA"], c["G"], c["HI"], c["HPG"], c["DA"]
    DBc, NGB, NCH, FC, PARTS, PMAX = c["DBc"], c["NGB"], c["NCH"], c["FC"], c["PARTS"], c["PMAX"]
    LW, LA, LG, DPLE, TOPK, NBIS, YBC = c["LW"], c["LA"], c["LG"], c["DPLE"], c["TOPK"], c["NBIS"], c["YBC"]
    LGC = LG // 128
    PC = DPLE // 128
    AKC = DA // 128
    vlay, NV = vec_layout(c)
    clay, NCF = cf_layout()

    nc = bass.Bass("TRN2", target_bir_lowering=False)
    Sc = Sched()
    es = ExitStack()

    def din(name, shape, dt=F32):
        return nc.dram_tensor(name, list(shape), dt, kind="ExternalInput").ap()

    dbg_out = {}

    def dscr(name, shape, dt):
        if name in dbg:
            t = nc.dram_tensor(name, list(shape), dt, kind="ExternalOutput")
            dbg_out[name] = t
        else:
            t = nc.dram_tensor(name, list(shape), dt)
        return t

    xT = din("xT", [D, S])
    xoT = din("xoT", [D, NTOK])
    pos = din("pos", [1, S], I32)
    poso = din("poso", [1, NTOK], I32)
    pT = din("pT", [DPLE, NTOK])
    vecs_d = din("vecs", [128, NV])
    cf_d = din("cf", [128, NCF])
    kpos_d = din("kpos", [1, S])
    w_kv = din("w_kv", [2 * G, 128, KC * 128])
    w_ki = din("w_ki", [1, 128, KC * 128])
    w_rkv = din("w_rkv", [3 * NGB, 128, KC * 128])
    w_l1 = din("w_l1", [1, 128, KC * LW])
    w_a1 = din("w_a1", [1, 128, KC * LA])
    w_g1 = din("w_g1", [LGC, 128, KC * 128])
    w2_d = din("w2", [LW, DBc])
    a2_d = din("a2", [LA, DBc])
    g2_d = din("g2", [128, LGC * DBc])
    w_q = din("w_q", [HA, 128, KC * 128])
    w_qi = din("w_qi", [HI, 128, KC * 128])
    w_wi = din("w_wi", [1, 128, KC * HI])
    w_pa = din("w_pa", [KC, 128, AKC * 128])
    w_pb = din("w_pb", [KC, 128, YBC * 128])
    w_g = din("w_g", [2 * KC, 128, KC * 128])
    w_o = din("w_o", [KC, 128, KC * 128])
    w_f1 = din("w_f1", [FC, 128, KC * 128])
    w_f3 = din("w_f3", [FC, 128, KC * 128])
    w_f2 = din("w_f2", [4 * KC, 128, PMAX * 128])
    w_pg = din("w_pg", [KC, 128, KC * 128])
    w_pl = din("w_pl", [KC, 128, PC * 128])
    outT = nc.dram_tensor("outT", [D, NTOK], F32, kind="ExternalOutput").ap()

    KT_t = dscr("KT", [G, 128, S], BF16)
    VT_t = dscr("Vtok", [S, G * 128], BF16)
    kiT_t = dscr("kiT", [128, S], BF16)
    rr_t = dscr("rr", [3 * NGB, 128, S], F32)
    wl_t = dscr("wl", [3 * NGB, 128, S], F32)
    qT_t = dscr("qT", [HA, 128, NTOK], BF16)
    qiT_t = dscr("qiT", [HI, 128, NTOK], BF16)
    wi_t = dscr("wi", [NTOK, HI], F32)
    xno_t = dscr("xno", [KC, 128, NTOK], BF16)
    yaT_t = dscr("yaT", [HA, 128, NTOK], BF16)
    NLG = 2 + LGC
    NGA1, NGO1 = 2 * G + 1 + 3 * NGB, HA + HI + 1
    wca = nc.dram_tensor("wca", [NGA1, 128, KC * 128], BF16).ap()
    wco = nc.dram_tensor("wco", [NGO1, 128, KC * 128], BF16).ap()
    B_wca, B_wco = Buf("wca", multi=True), Buf("wco", multi=True)
    wlp_t = nc.dram_tensor("wlp", [NLG, 128, 2 * KC * 128], BF16)
    wlp = wlp_t.ap()
    B_wlp = Buf("wlp", multi=True)
    ybs_t = [dscr("ybs%d" % n, [NGB * 128, T], BF16) for n in range(NT)]
    yba_t = [nc.dram_tensor("yba%d" % n, [4 * NGB * 128, T], BF16) for n in range(NT)]
    KT, VT, kiT, rr, wl, qT, qiT, wi_d, xno, yaT = [t.ap() for t in (
        KT_t, VT_t, kiT_t, rr_t, wl_t, qT_t, qiT_t, wi_t, xno_t, yaT_t)]
    ybs3 = [t.ap().rearrange("(g p) t -> g p t", g=NGB) for t in ybs_t]
    yba4 = [t.ap().rearrange("(r g p) t -> r g p t", r=4, g=NGB) for t in yba_t]
    B_KT, B_VT, B_kiT, B_rr, B_wl, B_q, B_wi, B_xno, B_ya, B_ybs, B_yba, B_out = [Buf(n, multi=True) for n in (
        "KT", "VT", "kiT", "rr", "wl", "q", "wi", "xno", "ya", "ybs", "yba", "out")]

    sbo = [16512]
    uid = [0]

    def sb(shape, dt, name="t"):
        esz = 4 if dt in (F32, I32) else 2
        n = 1
        for s_ in shape[1:]:
            n *= s_
        nbytes = (n * esz + 63) // 64 * 64
        uid[0] += 1
        t = nc.alloc_sbuf_tensor_at("%s_%d" % (name, uid[0]), list(shape), dt, offset=sbo[0])
        sbo[0] += nbytes
        assert sbo[0] <= 229344, ("SBUF overflow", name, sbo[0])
        return t

    PS = [es.enter_context(nc.psum_tensor("ps%d" % i, [128, 512], F32)) for i in range(8)]
    BPS = [Buf("ps%d" % i) for i in range(8)]

    def mm(out, lhsT, rhs, start, stop, R, W):
        Sc.op("pe", lambda e: e.matmul(out, lhsT=lhsT, rhs=rhs, start=start, stop=stop), reads=R, writes=W)

    def tr(out, in_, ident, R, W):
        Sc.op("pe", lambda e: e.transpose(out, in_, ident), reads=R, writes=W)

    def act(out, in_, func, R, W, bias=None, scale=None, accum=None):
        kw = {}
        if bias is not None:
            kw["bias"] = bias
        if scale is not None:
            kw["scale"] = scale
        if accum is not None:
            kw["accum_out"] = accum
        Sc.op("act", lambda e: e.activation(out, in_, func, **kw), reads=R, writes=W)

    def tt(out, in0, in1, op, R, W, eng="dve"):
        Sc.op(eng, lambda e: e.tensor_tensor(out, in0, in1, op), reads=R, writes=W)

    def ts(out, in0, s1, op0, R, W, s2=None, op1=None, accum=None, eng="dve"):
        kw = {}
        if op1 is not None:
            kw["op1"] = op1
        if accum is not None:
            kw["accum_out"] = accum
        Sc.op(eng, lambda e: e.tensor_scalar(out, in0, s1, s2, op0, **kw), reads=R, writes=W)

    def stt(out, in0, scalar, in1, op0, op1, R, W):
        Sc.op("dve", lambda e: e.scalar_tensor_tensor(out, in0, scalar, in1, op0, op1), reads=R, writes=W)

    def cp(out, in_, R, W, eng="dve"):
        Sc.op(eng, lambda e: e.tensor_copy(out, in_), reads=R, writes=W)

    def recip(out, in_, R, W):
        Sc.op("dve", lambda e: e.reciprocal(out, in_), reads=R, writes=W)

    def memset(ap, val, W, eng="dve"):
        Sc.op(eng, lambda e: e.memset(ap, val), writes=W)

    def dma(q, out, in_, R, W, key=None):
        Sc.op(q, lambda e: e.dma_start(out=out, in_=in_), reads=R, writes=W, dma=True, key=key)

    vecs = sb([128, NV], F32, "vecs")
    cf = sb([128, NCF], F32, "cf")
    cb = sb([128, NCF], BF16, "cb")
    omu = sb([128, 3 * KC], F32, "omu")
    omur = sb([128, 3 * NGB], F32, "omur")
    B_c = Buf("consts")
    dma("sp", vecs[:], vecs_d, [], [B_c])
    B_cf = Buf("cf")
    dma("sp", cf[:], cf_d, [], [B_cf])
    cp(cb[:], cf[:], [B_cf], [B_c])

    def V(name, i=None):
        o, w = vlay[name]
        if i is None:
            return vecs[:, o:o + w]
        return vecs[:, o + i:o + i + 1]

    def CF(name, rows=slice(0, 128)):
        o, w = clay[name]
        return cf[rows, o:o + w]

    def CB(name, rows=slice(0, 128)):
        o, w = clay[name]
        return cb[rows, o:o + w]

    o_, w_ = vlay["muw"]
    ts(omu[:], vecs[:, o_:o_ + 3 * KC], -1.0, ALU.mult, [B_c], [B_c], s2=1.0, op1=ALU.add)
    o_, w_ = vlay["mur"]
    ts(omur[:], vecs[:, o_:o_ + 3 * NGB], -1.0, ALU.mult, [B_c], [B_c], s2=1.0, op1=ALU.add)

    WMAXE = max(KC, AKC, YBC, PMAX) * 128
    NST, NWB = 2, 2
    wst = [sb([128, WMAXE], F32, "wst") for _ in range(NST)]
    wbf = [sb([128, WMAXE], BF16, "wbf") for _ in range(NWB)]
    B_wst = [Buf("wst%d" % i) for i in range(NST)]
    B_wbf = [tuple(Buf("wbf%d_%d" % (i, k)) for k in range(3)) for i in range(NWB)]
    wctr = [0, 0]
    CAST_FR = (0.45, 0.35)

    def wload(src, n):
        i = wctr[0] % NST
        wctr[0] += 1
        j = wctr[1] % NWB
        wctr[1] += 1
        Sc.op("sp", lambda e, o_=wst[i][:, :n], i_=src: e.dma_start(out=o_, in_=i_), reads=[], writes=[B_wst[i]],
              dma=True, nofloor=True)
        c1 = int(n * CAST_FR[0]) // 64 * 64
        c2 = c1 + int(n * CAST_FR[1]) // 64 * 64
        if c1 > 0:
            act(wbf[j][:, :c1], wst[i][:, :c1], AF.Copy, [B_wst[i]], [B_wbf[j][0]])
        if c2 > c1:
            cp(wbf[j][:, c1:c2], wst[i][:, c1:c2], [B_wst[i]], [B_wbf[j][1]])
        if n > c2:
            cp(wbf[j][:, c2:n], wst[i][:, c2:n], [B_wst[i]], [B_wbf[j][2]], eng="pool")
        return wbf[j], B_wbf[j]

    def wload_bf(src, n, rb=None):
        j = wctr[1] % NWB
        wctr[1] += 1
        Sc.op("sp", lambda e, o_=wbf[j][:, :n], i_=src: e.dma_start(out=o_, in_=i_),
              reads=[rb if rb is not None else B_wlp], writes=list(B_wbf[j]), dma=True, key=B_wbf[j][0], nofloor=True)
        return wbf[j], B_wbf[j]

    prefetched = [None]

    def gemm_stream(groups, consume, next_first=None):
        def ld(g):
            if len(g) > 2:
                return wload_bf(g[0], g[1], g[3] if len(g) > 3 else None)
            return wload(g[0], g[1])
        if prefetched[0] is not None:
            assert prefetched[0][2] == groups[0][1], (prefetched[0][2], groups[0][1])
            pend = prefetched[0][:2]
            prefetched[0] = None
        else:
            pend = ld(groups[0])
        for i in range(len(groups)):
            if i + 1 < len(groups):
                nxt = ld(groups[i + 1])
            elif next_first is not None:
                t_, b_ = ld(next_first)
                prefetched[0] = (t_, b_, next_first[1])
                nxt = None
            else:
                nxt = None
            consume(i, pend[0], pend[1])
            pend = nxt

    g_mark = sbo[0]

    def rmsnorm(xt, B_xt, outt, B_out, gname, sq, B_sq, rst, B_rst, bank, width=None, off=0):
        for kc in range(KC):
            i = kc % 2
            act(sq[i][:], xt[:, kc, :], AF.Square, [B_xt], [B_sq[i]])
            mm(PS[bank][:, :T], CF("ones"), sq[i][:], kc == 0, kc == KC - 1, [B_sq[i], B_cf], [BPS[bank]])
        act(rst[:], PS[bank][:, :T], AF.Sqrt, [BPS[bank]], [B_rst], bias=EPS_AP[:], scale=1.0 / D)
        recip(rst[:], rst[:], [B_rst], [B_rst])
        for kc in range(KC):
            if gname is not None:
                stt(outt[:, kc, off:off + T], xt[:, kc, :], V(gname, kc), rst[:], ALU.mult, ALU.mult,
                    [B_xt, B_rst, B_c], [B_out])
            else:
                tt(outt[:, kc, off:off + T], xt[:, kc, :], rst[:], ALU.mult, [B_xt, B_rst], [B_out])

    eps_t = sb([128, 2], F32, "eps")
    EPS_AP = eps_t[:, 0:1]
    memset(eps_t[:, 0:1], 1e-6, [B_c])
    memset(eps_t[:, 1:2], 64e-5, [B_c])
    GNEPS_AP = eps_t[:, 1:2]
    g_mark = sbo[0]

    TWO_PI = 6.283185

    def rope_tables(pos_ap, cosT, sinT, B_tab, tmp):
        pi_, u, ui, f1, f2 = tmp
        B_t = Buf("ropetmp")
        dma("sp", pi_[:], pos_ap.to_broadcast([128, T]), [], [B_t])
        ts(u[:], pi_[:], V("invf"), ALU.mult, [B_t, B_c], [B_t], s2=1.0 / (2 * math.pi), op1=ALU.mult)
        for which in (0, 1):
            if which == 1:
                ts(u[:], u[:], 0.25, ALU.add, [B_t], [B_t])
            cp(ui[:], u[:], [B_t], [B_t])
            tt(f1[:], u[:], ui[:], ALU.subtract, [B_t], [B_t])
            stt(f2[:], f1[:], 0.5, f1[:], ALU.is_gt, ALU.subtract, [B_t], [B_t])
            stt(f1[:], f2[:], 0.5, f2[:], ALU.is_gt, ALU.subtract, [B_t], [B_t])
            if which == 0:
                ts(f1[:], f1[:], V("sgn"), ALU.mult, [B_t, B_c], [B_t])
                act(sinT[:], f1[:], AF.Sin, [B_t], [B_tab], scale=TWO_PI)
            else:
                act(cosT[:], f1[:], AF.Sin, [B_t], [B_tab], scale=TWO_PI)

    def rope_epilogue(bank, cosT, sinT, B_tab, tq, tc, B_e, out_bf, B_o, rbank):
        act(tq[:], PS[bank][:, :T], AF.Copy, [BPS[bank]], [B_e])
        tt(tc[:], PS[bank][:, :T], cosT[:], ALU.mult, [BPS[bank], B_tab], [B_e])
        mm(PS[rbank][:, :T], CB("prot"), tq[:], True, True, [B_e, B_c], [BPS[rbank]])
        tt(tq_f[:], PS[rbank][:, :T], sinT[:], ALU.mult, [BPS[rbank], B_tab], [B_e2])
        tt(out_bf, tq_f[:], tc[:], ALU.add, [B_e2, B_e], [B_o])

    xt_off = sbo[0]
    xt = sb([128, KC, T], F32, "xt")
    B_xt = Buf("xt")
    lsc = nc.alloc_sbuf_tensor_at("lsc", [128, 2, KC * 128], BF16, offset=xt_off)
    assert 2 * KC * 128 * 2 <= KC * T * 4
    lgs = [(w_l1[0], LW, 0, "muw"), (w_a1[0], LA, KC, "mua")] + [(w_g1[k], 128, 2 * KC, "mug") for k in range(LGC)]
    for gi, (srcw, cw, mo, mun) in enumerate(lgs):
        i_ = wctr[0] % NST
        wctr[0] += 1
        dma("sp", wst[i_][:, :KC * cw], srcw, [], [B_wst[i_]])
        for kc in range(KC):
            eng_ = "dve" if kc % 2 == 0 else "pool"
            ts(lsc[:, 0, kc * cw:(kc + 1) * cw], wst[i_][:, kc * cw:(kc + 1) * cw], omu[:, mo + kc:mo + kc + 1], ALU.mult,
               [B_wst[i_], B_c], [B_xt], eng=eng_)
            ts(lsc[:, 1, kc * cw:(kc + 1) * cw], wst[i_][:, kc * cw:(kc + 1) * cw], V(mun, kc), ALU.mult,
               [B_wst[i_], B_c], [B_xt], eng=eng_)
        for v_ in range(2):
            dma("pool", wlp[gi, :, v_ * KC * 128:v_ * KC * 128 + KC * cw], lsc[:, v_, :KC * cw], [B_xt], [B_wlp], key=B_xt)
    xn = sb([128, KC, T + 1], BF16, "xn")
    B_xn = Buf("xn")
    sq = [sb([128, T], F32, "sq") for _ in range(2)]
    B_sq = [Buf("sq0"), Buf("sq1")]
    rst = sb([128, T], F32, "rst")
    B_rst = Buf("rst")
    cosT = sb([128, T], F32, "cos")
    sinT = sb([128, T], F32, "sin")
    B_tab = Buf("tab")
    rtmp = [sb([128, T], I32, "pi"), sb([128, T], F32, "u"), sb([128, T], I32, "ui"),
            sb([128, T], F32, "f1"), sb([128, T], F32, "f2")]
    tq = sb([128, T], BF16, "tq")
    tc = sb([128, T], F32, "tc")
    tq_f = sb([128, T], F32, "tqf")
    B_e, B_e2 = Buf("e"), Buf("e2")
    ob = [sb([128, T], BF16, "ob") for _ in range(2)]
    B_ob = [Buf("ob0"), Buf("ob1")]
    of = [sb([128, T], F32, "of") for _ in range(2)]
    B_of = [Buf("of0"), Buf("of1")]
    v4 = sb([128, G, T], BF16, "v4")
    B_v4 = Buf("v4")
    vtk = sb([128, G * 128], BF16, "vtk")
    B_vtk = Buf("vtk")
    xl = [sb([128, T], BF16, "xl") for _ in range(3)]
    B_xl = [Buf("xl%d" % i) for i in range(3)]
    xlt = [sb([128, T], F32, "xlt") for _ in range(2)]
    B_xlt = [Buf("xlt0"), Buf("xlt1")]
    hl = sb([128, LGC, T], BF16, "hl")
    B_hl = Buf("hl")
    w2s = sb([128, DBc], F32, "w2s")
    w2b = sb([128, 2 + LGC, DBc], BF16, "w2b")
    B_w2 = Buf("w2")
    dma("sp", w2s[:LW, :], w2_d, [], [B_w2])
    cp(w2b[:LW, 0, :], w2s[:LW, :], [B_w2], [B_w2])
    dma("sp", w2s[:LA, :], a2_d, [], [B_w2])
    cp(w2b[:LA, 1, :], w2s[:LA, :], [B_w2], [B_w2])
    for k in range(LGC):
        dma("sp", w2s[:, :], g2_d[:, k * DBc:(k + 1) * DBc], [], [B_w2])
        cp(w2b[:, 2 + k, :], w2s[:, :], [B_w2], [B_w2])
    octr = [0]

    s1_seq = [(n, False) for n in range(NT)] + [(n, True) for n in range(NOWN)]
    x_loaded = set()

    def load_x(si):
        if si in x_loaded or si >= len(s1_seq):
            return
        x_loaded.add(si)
        n, own = s1_seq[si]
        src = xoT if own else xT
        for q4 in range(4):
            ksl = slice(q4 * KC // 4, (q4 + 1) * KC // 4)
            dma("sp", xt[:, ksl, :], src.rearrange("(kc p) t -> p kc t", p=128)[:, ksl, n * T:(n + 1) * T], [], [B_xt])

    stats_done = set()

    def rms_stats(si):
        if si in stats_done or si >= len(s1_seq):
            return
        stats_done.add(si)
        load_x(si)
        for kc in range(KC):
            i = kc % 2
            act(sq[i][:], xt[:, kc, :], AF.Square, [B_xt], [B_sq[i]])
            mm(PS[0][:, :T], CF("ones"), sq[i][:], kc == 0, kc == KC - 1, [B_sq[i], B_cf], [BPS[0]])
        act(rst[:], PS[0][:, :T], AF.Sqrt, [BPS[0]], [B_rst], bias=EPS_AP[:], scale=1.0 / D)
        recip(rst[:], rst[:], [B_rst], [B_rst])

    def s1_tile(si_):
        n, own = s1_seq[si_]
        psrc = poso if own else pos
        load_x(si_)
        if n == 0 or own:
            memset(xn[:, :, 0:1], 0.0, [B_xn])
        else:
            cp(xn[:, :, 0:1], xn[:, :, T:T + 1], [B_xn], [B_xn])
        rms_stats(si_)
        for kc in range(KC):
            stt(xn[:, kc, 1:T + 1], xt[:, kc, :], V("nmix", kc), rst[:], ALU.mult, ALU.mult, [B_xt, B_rst, B_c], [B_xn])
        rope_tables(psrc[:, n * T:(n + 1) * T], cosT, sinT, B_tab, rtmp)
        if own:
            for kc in range(KC):
                dma("pool", xno[kc, :, n * T:(n + 1) * T], xn[:, kc, 1:T + 1], [B_xn], [B_xno], key=B_xn)
        groups = []
        kinds = []
        if own:
            for h in range(HA):
                groups.append((w_q[h], KC * 128)); kinds.append(("q", h))
            for h in range(HI):
                groups.append((w_qi[h], KC * 128)); kinds.append(("qi", h))
            groups.append((w_wi[0], KC * HI)); kinds.append(("wi", 0))
        else:
            for g in range(G):
                groups.append((w_kv[g], KC * 128)); kinds.append(("k", g))
            groups.append((w_ki[0], KC * 128)); kinds.append(("ki", 0))
            for g in range(G):
                groups.append((w_kv[G + g], KC * 128)); kinds.append(("v", g))
            for i in range(3 * NGB):
                groups.append((w_rkv[i], KC * 128)); kinds.append(("rkv", i))
            for gi_, (kd, idx_, cw_) in enumerate([("l1", 0, LW), ("a1", 0, LA)] + [("g1", k, 128) for k in range(LGC)]):
                for v_ in range(2):
                    groups.append((wlp[gi_, :, v_ * KC * 128:v_ * KC * 128 + KC * cw_], KC * cw_, "bf"))
                    kinds.append((kd, idx_ * 2 + v_))

        wc_, B_wc_ = (wco, B_wco) if own else (wca, B_wca)
        cache_n = [g_[1] for g_ in groups if len(g_) == 2]
        if n >= 1:
            groups = [((wc_[ci_][:, :g_[1]], g_[1], "bf", B_wc_) if len(g_) == 2 else g_) for ci_, g_ in enumerate(groups)]

        def consume(i, wt, wb):
            kind, idx = kinds[i]
            bank = 1 + (i % 2)
            t0, t1 = n * T, (n + 1) * T
            if n == 0 and i < len(cache_n):
                ne_ = cache_n[i]
                dma("pool", wc_[i][:, :ne_], wt[:, :ne_], [wb], [B_wc_], key=wb[0])
            if i == 2:
                load_x(si_ + 1)
            if i == 12:
                rms_stats(si_ + 1)
            if kind == "wi":
                w3 = wt[:, :KC * HI].rearrange("p (k c) -> p k c", c=HI)
                for tb in range(T // 128):
                    for kc in range(KC):
                        mm(PS[bank][:, tb * HI:(tb + 1) * HI], xn[:, kc, 1 + tb * 128:1 + (tb + 1) * 128], w3[:, kc, :],
                           kc == 0, kc == KC - 1, [B_xn, wb], [BPS[bank]])
                j = octr[0] % 2
                octr[0] += 1
                ts(of[j][:, :(T // 128) * HI], PS[bank][:, :(T // 128) * HI], float(HI ** -0.5 * 128 ** -0.5), ALU.mult,
                   [BPS[bank]], [B_of[j]])
                for tb in range(T // 128):
                    dma("pool", wi_d[n * T + tb * 128:n * T + (tb + 1) * 128, :], of[j][:, tb * HI:(tb + 1) * HI],
                        [B_of[j]], [B_wi], key=B_of[j])
                return
            if kind in ("l1", "a1", "g1"):
                cw = {"l1": LW, "a1": LA, "g1": 128}[kind]
                ver = idx % 2
                idx = idx // 2
                bank = 5
                w3 = wt[:, :KC * cw].rearrange("p (k c) -> p k c", c=cw)
                for kc in range(KC):
                    rhs_ = xn[:, kc, 1:T + 1] if ver == 0 else xn[:, kc, 0:T]
                    mm(PS[bank][:cw, :T], w3[:, kc, :], rhs_, ver == 0 and kc == 0, ver == 1 and kc == KC - 1,
                       [wb, B_xn], [BPS[bank]])
                if ver == 0:
                    return
                fn = {"l1": AF.Tanh, "a1": AF.Copy, "g1": AF.Sigmoid}[kind]
                act(hl[:cw, idx if kind == "g1" else 0, :], PS[bank][:cw, :T], fn, [BPS[bank]], [B_hl])
                if kind == "g1" and idx < LGC - 1:
                    return
                slot = {"l1": 0, "a1": 1, "g1": 2}[kind]
                for gb in range(NGB):
                    if kind == "g1":
                        for k in range(LGC):
                            mm(PS[3][:, :T], w2b[:, 2 + k, gb * 128:(gb + 1) * 128], hl[:, k, :], k == 0, k == LGC - 1,
                               [B_w2, B_hl], [BPS[3]])
                    else:
                        mm(PS[3][:, :T], w2b[:cw, slot, gb * 128:(gb + 1) * 128], hl[:cw, 0, :], True, True,
                           [B_w2, B_hl], [BPS[3]])
                    j = octr[0] % 2
                    octr[0] += 1
                    act(of[j][:], PS[3][:, :T], AF.Copy, [BPS[3]], [B_of[j]])
                    dma("pool", wl[slot * NGB + gb, :, t0:t1], of[j][:], [B_of[j]], [B_wl], key=B_of[j])
                return
            w3 = wt[:, :KC * 128].rearrange("p (k c) -> p k c", c=128)
            for kc in range(KC):
                mm(PS[bank][:, :T], w3[:, kc, :], xn[:, kc, 1:T + 1], kc == 0, kc == KC - 1, [wb, B_xn], [BPS[bank]])
            if kind in ("k", "ki", "q", "qi"):
                j = octr[0] % 2
                octr[0] += 1
                rope_epilogue(bank, cosT, sinT, B_tab, tq, tc, B_e, ob[j][:], B_ob[j], 3)
                if kind == "k":
                    dma("pool", KT[idx, :, t0:t1], ob[j][:], [B_ob[j]], [B_KT], key=B_ob[j])
                elif kind == "ki":
                    dma("pool", kiT[:, t0:t1], ob[j][:], [B_ob[j]], [B_kiT], key=B_ob[j])
                elif kind == "q":
                    dma("pool", qT[idx, :, t0:t1], ob[j][:], [B_ob[j]], [B_q], key=B_ob[j])
                else:
                    dma("pool", qiT[idx, :, t0:t1], ob[j][:], [B_ob[j]], [B_q], key=B_ob[j])
            elif kind == "v":
                act(v4[:, idx, :], PS[bank][:, :T], AF.Copy, [BPS[bank]], [B_v4])
                if idx == G - 1:
                    for tb in range(T // 128):
                        pv = PS[4][:].bitcast(BF16)
                        for g in range(G):
                            tr(pv[:, g * 128:(g + 1) * 128], v4[:, g, tb * 128:(tb + 1) * 128], CB("ident"),
                               [B_v4, B_c], [BPS[4]])
                        cp(vtk[:], pv[:, :G * 128], [BPS[4]], [B_vtk])
                        dma("pool", VT[t0 + tb * 128:t0 + (tb + 1) * 128, :], vtk[:], [B_vtk], [B_VT], key=B_vtk)
            elif kind == "rkv":
                j = octr[0] % 2
                octr[0] += 1
                act(of[j][:], PS[bank][:, :T], AF.Copy, [BPS[bank]], [B_of[j]])
                dma("pool", rr[idx, :, t0:t1], of[j][:], [B_of[j]], [B_rr], key=B_of[j])

        nf = None
        if si_ + 1 < len(s1_seq):
            n2_, own2_ = s1_seq[si_ + 1]
            if n2_ >= 1:
                nf = (wco[0], KC * 128, "bf", B_wco) if own2_ else (wca[0], KC * 128, "bf", B_wca)
            else:
                nf = (w_q[0], KC * 128) if own2_ else (w_kv[0], KC * 128)
        gemm_stream(groups, consume, next_first=nf)

    for si_ in range(len(s1_seq)):
        s1_tile(si_)

    Sc.barrier()
    sbo[0] = g_mark


    Sc.barrier()
    sbo[0] = g_mark
    NLEV = 5
    B_cc = Buf("cc")
    HS = [slice(0, 64), slice(64, 128)]

    def t3(t):
        return t[:].rearrange("p (c t) -> p c t", t=64)

    onesT = sb([128, T], F32, "onesT")
    memset(onesT[:], 1.0, [B_c])
    Rr = [sb([128, T + 1], F32, "Rr") for _ in range(3)]
    B_Rr = [Buf("Rr%d" % i) for i in range(3)]
    Ll = [sb([128, T], F32, "Ll") for _ in range(3)]
    B_Ll = [Buf("Ll%d" % i) for i in range(3)]
    NTMP = 10
    tm = [sb([128, T], F32, "tm") for _ in range(NTMP)]
    B_tm = [Buf("tm%d" % i) for i in range(NTMP)]
    vb16_2 = [sb([128, T], BF16, "vb16") for _ in range(2)]
    Bp_2 = [sb([128, T], BF16, "Bp") for _ in range(2)]
    Kp_2 = [sb([128, T], BF16, "Kp") for _ in range(2)]
    B_vb_2 = [Buf("vb16_0"), Buf("vb16_1")]
    Ng = [sb([128, T], BF16, "Ng") for _ in range(2)]
    NTg = [sb([128, T], BF16, "NTg") for _ in range(2)]
    Mg = [sb([128, T], BF16, "Mg") for _ in range(2)]
    B_Ng = [Buf("Ng0"), Buf("Ng1")]
    B_NTg = [Buf("NTg0"), Buf("NTg1")]
    B_Mg = [Buf("Mg0"), Buf("Mg1")]
    GT = []
    for gb in range(NGB):
        d = {}
        for nm, shp, dt in (("AR", [128, NCH, 128], BF16), ("BT", [128, T], BF16), ("KTl", [128, T], BF16),
                            ("NB", [128, NCH, 128], BF16), ("NK", [128, NCH, 128], BF16),
                            ("Minv", [128, T], BF16), ("Vtk", [128, T], BF16), ("Btk", [128, T], BF16),
                            ("Ktk", [128, T], BF16), ("WC", [128, NCH], F32), ("vfm", [128, T], F32),
                            ("gfm", [128, T], F32), ("rkS", [128, T], F32), ("Tst", [128, 64], F32),
                            ("Tb0", [128, 64], BF16), ("Tb1", [128, 64], BF16), ("Zs", [128, 64], BF16),
                            ("Ps", [128, 64], BF16), ("YT", [128, T], F32)):
            d[nm] = sb(shp, dt, nm)
            d["B_" + nm] = Buf(nm + str(gb))
        for rg in ("z", "p", "y", "t"):
            d["B_ps" + rg] = Buf("ps" + rg + str(gb))
        GT.append(d)
        memset(d["Tst"][:], 0.0, [d["B_Tst"]])
        memset(d["Tb0"][:], 0.0, [d["B_Tb0"]])
    yo = [sb([128, T], BF16, "yo") for _ in range(2)]
    B_yo = [Buf("yo0"), Buf("yo1")]
    CEXP = -math.exp(-0.5)

    def rw_prep(n, gb):
        d = GT[gb]
        vb16, Bp, Kp, B_vb = vb16_2[gb % 2], Bp_2[gb % 2], Kp_2[gb % 2], B_vb_2[gb % 2]
        t0, t1 = n * T, (n + 1) * T
        for i in range(3):
            if n == 0:
                memset(Rr[i][:, 0:1], 0.0, [B_Rr[i]])
                dma("sp", Rr[i][:, 1:T + 1], rr[i * NGB + gb, :, t0:t1], [B_rr], [B_Rr[i]])
            else:
                dma("sp", Rr[i][:, :], rr[i * NGB + gb, :, t0 - 1:t1], [B_rr], [B_Rr[i]])
            dma("sp", Ll[i][:], wl[i * NGB + gb, :, t0:t1], [B_wl], [B_Ll[i]])
        r_, k_, tmpa, sig, cs, Wt, Winv, Wp, a_, kk = tm
        Br, Bk, Bta, Bsig, Bcs, BWt, BWinv, BWp, Ba, Bkk = B_tm
        vfm = d["vfm"]
        for (src, Bs, dst, Bd, mi, mun) in ((Rr[0], B_Rr[0], r_, Br, 0, "mur"), (Rr[1], B_Rr[1], k_, Bk, 1, "muk"),
                                            (Rr[2], B_Rr[2], vfm, d["B_vfm"], 2, "muv")):
            ts(tmpa[:], src[:, 1:T + 1], omur[:, mi * NGB + gb:mi * NGB + gb + 1], ALU.mult, [Bs, B_c], [Bta])
            stt(dst[:], src[:, 0:T], V(mun, gb), tmpa[:], ALU.mult, ALU.add, [Bs, B_c, Bta], [Bd])
        cp(vb16[:], vfm[:], [d["B_vfm"]], [B_vb], eng="pool")
        cp(d["gfm"][:], Ll[2][:], [B_Ll[2]], [d["B_gfm"]], eng="pool")
        act(sig[:], Ll[0][:], AF.Sigmoid, [B_Ll[0], B_c], [Bsig], bias=V("w0", gb))
        ts(sig[:], sig[:], CEXP, ALU.mult, [Bsig], [Bsig])
        Sc.op("dve", lambda e: e.tensor_tensor_scan(cs[:], onesT[:], sig[:], 0.0, ALU.mult, ALU.add),
              reads=[Bsig, B_c], writes=[Bcs])
        if NCH > 1:
            tt(t3(tmpa)[:, 1:, :], t3(cs)[:, 1:, :], t3(cs)[:, 0:NCH - 1, 63:64].to_broadcast([128, NCH - 1, 64]),
               ALU.subtract, [Bcs], [Bta])
        cp(t3(tmpa)[:, 0, :], t3(cs)[:, 0, :], [Bcs], [Bta])
        act(Wt[:], tmpa[:], AF.Exp, [Bta], [BWt])
        act(Winv[:], tmpa[:], AF.Exp, [Bta], [BWinv], scale=-1.0)
        tt(tmpa[:], tmpa[:], sig[:], ALU.subtract, [Bta, Bsig], [Bta])
        act(Wp[:], tmpa[:], AF.Exp, [Bta], [BWp])
        cp(d["WC"][:], t3(Wt)[:, :, 63], [BWt], [d["B_WC"]])
        act(a_[:], Ll[1][:], AF.Sigmoid, [B_Ll[1], B_c], [Ba], bias=V("a0", gb))
        ts(kk[:], k_[:], V("kk", gb), ALU.mult, [Bk, B_c], [Bkk])
        tt(tmpa[:], kk[:], kk[:], ALU.mult, [Bkk], [Bta])
        mm(PS[5][:, :T], CF("blk"), tmpa[:], True, True, [Bta, B_cf], [BPS[5]])
        act(tmpa[:], PS[5][:, :T], AF.Sqrt, [BPS[5]], [Bta])
        ts(tmpa[:], tmpa[:], 1e-12, ALU.max, [Bta], [Bta])
        recip(tmpa[:], tmpa[:], [Bta], [Bta])
        tt(kk[:], kk[:], tmpa[:], ALU.mult, [Bkk, Bta], [Bkk])
        ts(tmpa[:], a_[:], -1.0, ALU.add, [Ba, B_c], [Bta], s2=V("ka", gb), op1=ALU.mult)
        stt(k_[:], tmpa[:], 1.0, k_[:], ALU.add, ALU.mult, [Bta, Bk], [Bk])
        stt(tmpa[:], r_[:], V("rk", gb), k_[:], ALU.mult, ALU.mult, [Br, Bk, B_c], [Bta])
        mm(PS[5][:, :T], CF("blk"), tmpa[:], True, True, [Bta, B_cf], [BPS[5]])
        act(d["rkS"][:], PS[5][:, :T], AF.Copy, [BPS[5]], [d["B_rkS"]])
        AR3 = d["AR"]
        stt(AR3[:, :, 0:64], t3(kk), -1.0, t3(Wp), ALU.mult, ALU.mult, [Bkk, BWp], [d["B_AR"]])
        tt(AR3[:, :, 64:128], t3(r_), t3(Wt), ALU.mult, [Br, BWt], [d["B_AR"]])
        tt(tmpa[:], kk[:], a_[:], ALU.mult, [Bkk, Ba], [Bta])
        tt(tmpa[:], tmpa[:], Winv[:], ALU.mult, [Bta, BWinv], [Bta])
        cp(d["BT"][:], tmpa[:], [Bta], [d["B_BT"]], eng="pool")
        tt(t3(Bp), t3(tmpa), d["WC"][:, :].unsqueeze(2).to_broadcast([128, NCH, 64]), ALU.mult,
           [Bta, d["B_WC"]], [B_vb])
        tt(sig[:], k_[:], Winv[:], ALU.mult, [Bk, BWinv], [Bsig])
        cp(d["KTl"][:], sig[:], [Bsig], [d["B_KTl"]], eng="pool")
        tt(t3(Kp), t3(sig), d["WC"][:, :].unsqueeze(2).to_broadcast([128, NCH, 64]), ALU.mult,
           [Bsig, d["B_WC"]], [B_vb])
    def rw_prep2(n, gb):
        d = GT[gb]
        vb16, Bp, Kp, B_vb = vb16_2[gb % 2], Bp_2[gb % 2], Kp_2[gb % 2], B_vb_2[gb % 2]
        AR3 = d["AR"]
        BT3, KT3 = t3(d["BT"]), t3(d["KTl"])
        for (lt3, Bl, dst, Bd, bank0) in ((BT3, d["B_BT"], d["NB"], d["B_NB"], 0), (KT3, d["B_KTl"], d["NK"], d["B_NK"], 1)):
            for c0 in range(0, NCH, 4):
                n4 = min(4, NCH - c0)
                bank = bank0
                for cc in range(n4):
                    for hs in HS:
                        mm(PS[bank][hs, cc * 128:(cc + 1) * 128], lt3[hs, c0 + cc, :], AR3[hs, c0 + cc, :], True, True,
                           [Bl, d["B_AR"]], [BPS[bank]])
                tt(dst[:, c0:c0 + n4, :], PS[bank][:, :n4 * 128].rearrange("p (c t) -> p c t", t=128),
                   CF("mk").unsqueeze(1).to_broadcast([128, n4, 128]), ALU.mult, [BPS[bank], B_cf], [Bd])
        for cc in range(NCH):
            for hs in HS:
                mm(PS[2][hs, cc * 64:(cc + 1) * 64], AR3[hs, cc, 0:64], BT3[hs, cc, :], True, True,
                   [d["B_AR"], d["B_BT"]], [BPS[2]])
        tt(t3(NTg[0]), PS[2][:, :T].rearrange("p (c t) -> p c t", t=64),
           CF("mkl").unsqueeze(1).to_broadcast([128, NCH, 64]), ALU.mult, [BPS[2], B_cf], [B_NTg[0]])
        NB3 = d["NB"]
        tt(t3(Mg[0]), NB3[:, :, 0:64], CF("i64").unsqueeze(1).to_broadcast([128, NCH, 64]), ALU.add,
           [d["B_NB"], B_cf], [B_Mg[0]])
        curN, BcurN = NB3[:, :, 0:64], d["B_NB"]
        for lev in range(1, NLEV + 1):
            pi_, ci = (lev - 1) % 2, lev % 2
            last = lev == NLEV
            NTp = t3(NTg[pi_])
            for cc in range(NCH):
                for hs in HS:
                    if not last:
                        mm(PS[3][hs, cc * 64:(cc + 1) * 64], NTp[hs, cc, :], curN[hs, cc, :], True, True,
                           [B_NTg[pi_], BcurN], [BPS[3]])
                    mm(PS[4][hs, cc * 64:(cc + 1) * 64], curN[hs, cc, :], NTp[hs, cc, :], True, True,
                       [B_NTg[pi_], BcurN], [BPS[4]])
            if not last:
                act(Ng[ci][:], PS[3][:, :T], AF.Copy, [BPS[3]], [B_Ng[ci]])
            cp(NTg[ci][:], PS[4][:, :T], [BPS[4]], [B_NTg[ci]])
            NTc = t3(NTg[ci])
            Mp = t3(Mg[pi_])
            for cc in range(NCH):
                for hs in HS:
                    mm(PS[2][hs, cc * 64:(cc + 1) * 64], NTc[hs, cc, :], Mp[hs, cc, :], True, True,
                       [B_NTg[ci], B_Mg[pi_]], [BPS[2]])
            dstM, BdM = (d["Minv"], d["B_Minv"]) if last else (Mg[ci], B_Mg[ci])
            tt(dstM[:], PS[2][:, :T], Mg[pi_][:], ALU.add, [BPS[2], B_Mg[pi_]], [BdM])
            if not last:
                curN, BcurN = t3(Ng[ci]), B_Ng[ci]
        for ti, (src, dst, Bd) in enumerate(((vb16, d["Vtk"], d["B_Vtk"]), (Bp, d["Btk"], d["B_Btk"]),
                                             (Kp, d["Ktk"], d["B_Ktk"]))):
            tb_ = ti % 2
            pvw = PS[tb_][:].bitcast(BF16)
            s3 = t3(src)
            for cc in range(NCH):
                for hi, hs in enumerate(HS):
                    o_, w_ = clay["ident"]
                    tr(pvw[hs, cc * 64:(cc + 1) * 64], s3[hs, cc, :], cb[hs, o_ + hi * 64:o_ + hi * 64 + 64],
                       [B_vb, B_c], [BPS[tb_]])
            cp(dst[:], pvw[:, :T], [BPS[tb_]], [Bd])

    def rw_seq(n):
        for cc in range(NCH):
            gi = n * NCH + cc
            cur, nxt = "Tb%d" % (gi % 2), "Tb%d" % ((gi + 1) % 2)
            for gb in range(NGB):
                d = GT[gb]
                AR3, NB3, NK3 = d["AR"], d["NB"], d["NK"]
                Vt3, Bt3, Kt3, Mi3 = t3(d["Vtk"]), t3(d["Btk"]), t3(d["Ktk"]), t3(d["Minv"])
                zc = slice(gb * 128, gb * 128 + 64)
                pc = slice(gb * 128 + 64, gb * 128 + 128)
                for hs in HS:
                    mm(PS[6][hs, zc], AR3[hs, cc, 0:64], d[cur][hs, :], True, False, [d["B_AR"], d["B_" + cur]], [d["B_psz"]])
                    mm(PS[6][hs, zc], NK3[hs, cc, 0:64], Vt3[hs, cc, :], False, True, [d["B_NK"], d["B_Vtk"]], [d["B_psz"]])
                act(d["Zs"][:], PS[6][:, zc], AF.Copy, [d["B_psz"]], [d["B_Zs"]])
                for hs in HS:
                    mm(PS[6][hs, pc], Mi3[hs, cc, :], d["Zs"][hs, :], True, True, [d["B_Minv"], d["B_Zs"]], [d["B_psp"]])
                cp(d["Ps"][:], PS[6][:, pc], [d["B_psp"]], [d["B_Ps"]])
                for hs in HS:
                    mm(PS[7][hs, zc], d[cur][hs, :], AR3[hs, cc, 64:128], True, False, [d["B_AR"], d["B_" + cur]], [d["B_psy"]])
                    mm(PS[7][hs, zc], d["Ps"][hs, :], NB3[hs, cc, 64:128], False, False, [d["B_Ps"], d["B_NB"]], [d["B_psy"]])
                    mm(PS[7][hs, zc], Vt3[hs, cc, :], NK3[hs, cc, 64:128], False, True, [d["B_Vtk"], d["B_NK"]], [d["B_psy"]])
                act(d["YT"][:, cc * 64:(cc + 1) * 64], PS[7][:, zc], AF.Copy, [d["B_psy"]], [d["B_YT"]])
                for hs in HS:
                    mm(PS[7][hs, pc], Bt3[hs, cc, :], d["Ps"][hs, :], True, False, [d["B_Btk"], d["B_Ps"]], [d["B_pst"]])
                    mm(PS[7][hs, pc], Kt3[hs, cc, :], Vt3[hs, cc, :], False, True, [d["B_Ktk"], d["B_Vtk"]], [d["B_pst"]])
                stt(d["Tst"][:], d["Tst"][:], d["WC"][:, cc:cc + 1], PS[7][:, pc], ALU.mult, ALU.add,
                    [d["B_Tst"], d["B_WC"], d["B_pst"]], [d["B_Tst"]])
                cp(d[nxt][:], d["Tst"][:], [d["B_Tst"]], [d["B_" + nxt]])

    def rw_epi(n, gb):
        d = GT[gb]
        e0, e1, e2 = tm[0], tm[1], tm[2]
        Be0, Be1, Be2 = B_tm[0], B_tm[1], B_tm[2]
        YT = d["YT"]
        mm(PS[5][:, :T], CF("blk"), YT[:], True, True, [d["B_YT"], B_cf], [BPS[5]])
        ts(e0[:], PS[5][:, :T], 1.0 / 64, ALU.mult, [BPS[5]], [Be0])
        tt(e1[:], YT[:], e0[:], ALU.subtract, [d["B_YT"], Be0], [Be1])
        tt(e2[:], e1[:], e1[:], ALU.mult, [Be1], [Be2])
        mm(PS[5][:, :T], CF("blk"), e2[:], True, True, [Be2, B_cf], [BPS[5]])
        act(e2[:], PS[5][:, :T], AF.Sqrt, [BPS[5]], [Be2], bias=GNEPS_AP, scale=1.0 / 64)
        recip(e2[:], e2[:], [Be2], [Be2])
        tt(e1[:], e1[:], e2[:], ALU.mult, [Be1, Be2], [Be1])
        ts(e1[:], e1[:], V("lnw", gb), ALU.mult, [Be1, B_c], [Be1], s2=V("lnb", gb), op1=ALU.add)
        tt(e0[:], d["rkS"][:], d["vfm"][:], ALU.mult, [d["B_rkS"], d["B_vfm"]], [Be0])
        tt(e1[:], e1[:], e0[:], ALU.add, [Be1, Be0], [Be1])
        j = (n * NGB + gb) % 2
        tt(yo[j][:], e1[:], d["gfm"][:], ALU.mult, [Be1, d["B_gfm"]], [B_yo[j]])
        dma("pool", ybs3[n][gb, :, :], yo[j][:], [B_yo[j]], [B_ybs], key=B_yo[j])

    for n in range(NT):
        rw_prep(n, 0)
        for gb in range(NGB):
            l2 = Sc.capture(lambda: rw_prep2(n, gb))
            l1 = Sc.capture(lambda: rw_prep(n, gb + 1)) if gb + 1 < NGB else []
            Sc.replay_interleaved(l2, l1)
        rw_seq(n)
        for gb in range(NGB):
            rw_epi(n, gb)
        if "nocc" not in dbg:
            Sc.op("pool", lambda e, n=n: e.collective_compute("AllGather", ALU.bypass,
                                                              replica_groups=[[0, 1, 2, 3], [4, 5, 6, 7]],
                                                              ins=[ybs_t[n].ap().opt()], outs=[yba_t[n].ap().opt()]),
                  reads=[B_ybs], writes=[B_yba], dma=True, key=B_cc, amt=1)

    Sc.barrier()
    sbo[0] = g_mark
    LMAX = S
    KCH = min(2048, 4 * T)
    QPT = T // 128
    qiTt = sb([128, HI, 128], BF16, "qiTt")
    qTt = sb([128, HA, 128], BF16, "qTt")
    wit = sb([128, HI], F32, "wit")
    B_ql = Buf("ql")
    kit = sb([128, LMAX], BF16, "kit")
    B_kit = Buf("kit")
    acc = sb([128, LMAX], F32, "acc")
    B_acc = Buf("acc")
    rl = [sb([128, 512], F32, "rl") for _ in range(2)]
    B_rl = [Buf("rl0"), Buf("rl1")]
    kpw = sb([128, 4 * T], F32, "kpw")
    B_kpw = Buf("kpw")
    junk = sb([128, LMAX], BF16, "junk")
    B_junk = Buf("junk")
    maskT = sb([128, LMAX // 128, 128], BF16, "maskT")
    B_mT = Buf("maskT")
    Kc = [sb([128, KCH], BF16, "Kc") for _ in range(2)]
    Vc = [sb([128, KCH // 128, 128], BF16, "Vc") for _ in range(2)]
    B_Kc = [Buf("Kc0"), Buf("Kc1")]
    B_Vc = [Buf("Vc0"), Buf("Vc1")]
    PT = [sb([128, HPG * 128], BF16, "PT") for _ in range(3)]
    B_PT = [Buf("PT%d" % i) for i in range(3)]
    dn = sb([128, HPG * 128], F32, "dn")
    B_dn = Buf("dn")
    yob = [sb([128, HPG * 128], BF16, "yob") for _ in range(2)]
    B_yob = [Buf("yob0"), Buf("yob1")]
    bs = sb([128, 8], F32, "bs")
    B_bs = Buf("bs")
    lo_, hi_, wd_, mid_, cnt_, pz_ = [bs[:, i:i + 1] for i in range(6)]
    kcc = [0]
    ptc = [0]
    yoc = [0]
    NQ = HPG * 128

    qTt2 = [qTt, sb([128, HA, 128], BF16, "qTt1")]
    B_qt = [Buf("qt0"), Buf("qt1")]
    LOOK = min(6, KCH // 128)
    NPT = LOOK + 2
    while len(PT) < NPT:
        PT.append(sb([128, HPG * 128], BF16, "PT"))
        B_PT.append(Buf("PT%d" % len(B_PT)))

    def qgeom(qb):
        m_ = qb // QPT
        L = (4 * m_ + 4) * T
        return L, L // 512, L // 128, qb * 128, (qb + 1) * 128

    def idx_phase(qb):
        L, NKT, NKB, q0, q1 = qgeom(qb)
        dma("sp", qiTt[:], qiT[:, :, q0:q1].rearrange("h p t -> p h t"), [B_q], [B_ql])
        dma("sp", qTt2[qb % 2][:], qT[:, :, q0:q1].rearrange("h p t -> p h t"), [B_q], [B_qt[qb % 2]])
        dma("sp", wit[:], wi_d[q0:q1, :], [B_wi], [B_ql])
        if qb % QPT == 0:
            dma("sp", kit[:, :L], kiT[:, 0:L], [B_kiT], [B_kit])
            dma("sp", kpw[:], kpos_d[:, L - 4 * T:L].to_broadcast([128, 4 * T]), [], [B_kpw])
        for kt in range(NKT):
            ks = slice(kt * 512, (kt + 1) * 512)
            for h in range(HI):
                b_ = h % 2
                mm(PS[b_][:, :512], qiTt[:, h, :], kit[:, ks], True, True, [B_ql, B_kit], [BPS[b_]])
                act(rl[b_][:], PS[b_][:, :512], AF.Relu, [BPS[b_]], [B_rl[b_]])
                if h == 0:
                    ts(acc[:, ks], rl[b_][:], wit[:, 0:1], ALU.mult, [B_rl[b_], B_ql], [B_acc])
                else:
                    stt(acc[:, ks], rl[b_][:], wit[:, h:h + 1], acc[:, ks], ALU.mult, ALU.add,
                        [B_rl[b_], B_ql, B_acc], [B_acc])
        Sc.op("dve", lambda e, L=L: e.tensor_reduce(hi_, acc[:, :L], mybir.AxisListType.X, ALU.max,
                                                    apply_absolute_value=True),
              reads=[B_acc], writes=[B_bs])
        ts(hi_, hi_, 1.0001, ALU.mult, [B_bs], [B_bs], s2=1e-6, op1=ALU.add)
        ts(lo_, hi_, -1.0, ALU.mult, [B_bs], [B_bs])
        ts(wd_, hi_, 2.0, ALU.mult, [B_bs], [B_bs])
        for i4 in range(4 * T // 512):
            b_ = i4 % 2
            ts(rl[b_][:], kpw[:, i4 * 512:(i4 + 1) * 512], V("qpos", qb), ALU.is_gt, [B_kpw, B_c], [B_rl[b_]],
               s2=-1e30, op1=ALU.mult)
            a0_ = L - 4 * T + i4 * 512
            tt(acc[:, a0_:a0_ + 512], acc[:, a0_:a0_ + 512], rl[b_][:], ALU.add, [B_acc, B_rl[b_]], [B_acc])

    B_junkA = Buf("junkA")

    def bis_iter(qb, it):
        L = qgeom(qb)[0]
        ck = 0.5 ** (it + 1)
        stt(mid_, wd_, ck, lo_, ALU.mult, ALU.add, [B_bs], [B_bs])
        ts(junk[:, :L], acc[:, :L], mid_, ALU.is_ge, [B_acc, B_bs], [B_junk, B_bs], op1=ALU.add, accum=cnt_)
        ts(pz_, cnt_, TOPK - 0.5, ALU.is_ge, [B_bs], [B_bs], s2=ck, op1=ALU.mult)
        stt(lo_, pz_, wd_, lo_, ALU.mult, ALU.add, [B_bs], [B_bs])

    def mask_phase(qb):
        L, NKT, NKB, q0, q1 = qgeom(qb)
        ts(junk[:, :L], acc[:, :L], lo_, ALU.is_lt, [B_acc, B_bs], [B_junk, B_junkA], s2=-30000.0, op1=ALU.mult)
        pvb = PS[2][:].bitcast(BF16)
        for k0 in range(0, NKB, 8):
            n8 = min(8, NKB - k0)
            for kk_ in range(n8):
                tr(pvb[:, kk_ * 128:(kk_ + 1) * 128], junk[:, (k0 + kk_) * 128:(k0 + kk_ + 1) * 128], CB("ident"),
                   [B_junk, B_junkA, B_c], [BPS[2]])
            cp(maskT[:, k0:k0 + n8, :], pvb[:, :n8 * 128].rearrange("p (k q) -> p k q", q=128), [BPS[2]], [B_mT])

    def attn_ops(qb):
        L, NKT, NKB, q0, q1 = qgeom(qb)
        qt, Bq = qTt2[qb % 2], B_qt[qb % 2]
        ops = []
        for g in range(G):
            nchunks = (L + KCH - 1) // KCH
            steps = []
            for kc_ in range(nchunks):
                k0 = kc_ * KCH
                kn = min(KCH, L - k0)
                for kb in range(kn // 128):
                    steps.append((kc_, k0, kn, kb))
            chunk_buf = {}
            st_info = {}

            def front(si, g=g, steps=steps, chunk_buf=chunk_buf, st_info=st_info):
                kc_, k0, kn, kb = steps[si]
                if kb == 0:
                    ci = kcc[0] % 2
                    kcc[0] += 1
                    chunk_buf[kc_] = ci
                    dma("sp", Kc[ci][:, :kn], KT[g, :, k0:k0 + kn], [B_KT], [B_Kc[ci]])
                    dma("sp", Vc[ci][:, :kn // 128, :],
                        VT[k0:k0 + kn, g * 128:(g + 1) * 128].rearrange("(k p) d -> p k d", p=128), [B_VT], [B_Vc[ci]])
                ci = chunk_buf[kc_]
                kbg = k0 // 128 + kb
                sbk = (3, 4, 7)[kbg % 3]
                pj = ptc[0] % NPT
                ptc[0] += 1
                st_info[si] = (ci, pj)
                mm(PS[sbk][:, :NQ], Kc[ci][:, kb * 128:(kb + 1) * 128],
                   qt[:, g * HPG:(g + 1) * HPG, :], True, False, [B_Kc[ci], Bq], [BPS[sbk]])
                mm(PS[sbk][:, :NQ], CB("ident"), maskT[:, kbg, :].unsqueeze(1).to_broadcast([128, HPG, 128]),
                   False, True, [B_c, B_mT], [BPS[sbk]])
                act(PT[pj][:], PS[sbk][:, :NQ], AF.Exp, [BPS[sbk]], [B_PT[pj]], scale=float(128 ** -0.5))

            def back(si, g=g, steps=steps, st_info=st_info):
                kc_, k0, kn, kb = steps[si]
                ci, pj = st_info[si]
                kbg = k0 // 128 + kb
                first, last_ = (kbg == 0), (kbg == NKB - 1)
                mm(PS[5][:, :NQ], Vc[ci][:, kb, :], PT[pj][:], first, last_, [B_Vc[ci], B_PT[pj]], [BPS[5]])
                mm(PS[6][:, :NQ], CB("ones"), PT[pj][:], first, last_, [B_c, B_PT[pj]], [BPS[6]])

            def fin(g=g):
                recip(dn[:], PS[6][:, :NQ], [BPS[6]], [B_dn])
                yj = yoc[0] % 2
                yoc[0] += 1
                tt(yob[yj][:], PS[5][:, :NQ], dn[:], ALU.mult, [BPS[5], B_dn], [B_yob[yj]])
                dma("pool", yaT[g * HPG:(g + 1) * HPG, :, q0:q1].rearrange("h p t -> p h t"),
                    yob[yj][:].rearrange("p (h q) -> p h q", q=128), [B_yob[yj]], [B_ya], key=B_yob[yj])

            def step(si, front=front, back=back, nst=len(steps)):
                if si == 0:
                    for s2_ in range(min(LOOK, nst)):
                        front(s2_)
                if si + LOOK < nst:
                    front(si + LOOK)
                back(si)

            for si in range(len(steps)):
                ops.append(lambda si=si, step=step: step(si))
            ops.append(fin)
        return ops

    for qb in range(NQB + 1):
        if qb < NQB:
            idx_phase(qb)
        aops = attn_ops(qb - 1) if qb >= 1 else []
        nb = NBIS if qb < NQB else 0
        if nb and aops:
            stride = max(1, len(aops) // nb)
        bi = 0
        for ai, o in enumerate(aops):
            o()
            if nb and bi < nb and (ai + 1) % stride == 0:
                bis_iter(qb, bi)
                bi += 1
        while bi < nb:
            bis_iter(qb, bi)
            bi += 1
        if qb < NQB:
            mask_phase(qb)
    Sc.barrier()
    sbo[0] = g_mark
    r1_off = sbo[0]
    xn4 = sb([128, KC, T], BF16, "xn4")
    ya4 = sb([128, AKC, T], BF16, "ya4")
    yb4 = sb([128, YBC, T], BF16, "yb4")
    r1_end = sbo[0]
    sbo[0] = r1_off
    resid = sb([128, KC, T], F32, "resid")
    sbo[0] = max(sbo[0], r1_end)
    B_R1 = Buf("R1")
    mixt = sb([128, KC, T], BF16, "mixt")
    B_mix = Buf("mix")
    h_off = sbo[0]
    hid = sb([128, PMAX, T], BF16, "hid")
    B_hid = Buf("hid")
    h_end = sbo[0]
    sbo[0] = h_off
    ysel = [sb([128, NGB, T], BF16, "ysel") for _ in range(4)]
    sbo[0] = max(sbo[0], h_end)
    B_ysel = [Buf("ysel%d" % i) for i in range(4)]
    sq4 = [sb([128, T], F32, "sq4") for _ in range(2)]
    B_sq4 = [Buf("sq40"), Buf("sq41")]
    rst4 = sb([128, T], F32, "rst4")
    B_rst4 = Buf("rst4")
    ea = [sb([128, T], F32, "ea") for _ in range(2)]
    eb = [sb([128, T], F32, "eb") for _ in range(2)]
    B_ea = [Buf("ea0"), Buf("ea1")]
    B_eb = [Buf("eb0"), Buf("eb1")]
    ptf = sb([128, PC, T], F32, "ptf")
    ptb = sb([128, PC, T], BF16, "ptb")
    B_pt = Buf("pt")
    fo = [sb([128, T], F32, "fo") for _ in range(2)]
    B_fo = [Buf("fo0"), Buf("fo1")]
    ec = [0]

    def rms_rstd(xt_, B_x):
        for kc in range(KC):
            i = kc % 2
            act(sq4[i][:], xt_[:, kc, :], AF.Square, [B_x], [B_sq4[i]])
            mm(PS[0][:, :T], CF("ones"), sq4[i][:], kc == 0, kc == KC - 1, [B_sq4[i], B_cf], [BPS[0]])
        act(rst4[:], PS[0][:, :T], AF.Sqrt, [BPS[0]], [B_rst4], bias=EPS_AP, scale=1.0 / D)
        recip(rst4[:], rst4[:], [B_rst4], [B_rst4])

    for m_ in range(NOWN):
        tsl = slice(m_ * T, (m_ + 1) * T)
        Sc.barrier()
        for q4 in range(4):
            ksl = slice(q4 * KC // 4, (q4 + 1) * KC // 4)
            dma("sp", xn4[:, ksl, :], xno[ksl, :, tsl].rearrange("k p t -> p k t"), [B_xno], [B_R1])
        dma("sp", ya4[:], yaT[:, :, tsl].rearrange("h p t -> p h t"), [B_ya], [B_R1])
        for r_ in range(4):
            for s_ in range(4):
                dma("sp", ysel[s_][:], yba4[4 * m_ + s_][r_].rearrange("g p t -> p g t"), [B_yba], [B_ysel[s_]])
            dst = yb4[:, r_ * NGB:(r_ + 1) * NGB, :]
            ts(dst, ysel[0][:], V("sel", 0), ALU.mult, [B_ysel[0], B_c], [B_R1])
            for s_ in range(1, 4):
                stt(dst, ysel[s_][:], V("sel", s_), dst, ALU.mult, ALU.add, [B_ysel[s_], B_c, B_R1], [B_R1])
        groups, kinds = [], []
        for c_ in range(KC):
            groups += [(w_pa[c_], AKC * 128), (w_pb[c_], YBC * 128), (w_g[c_], KC * 128), (w_g[KC + c_], KC * 128)]
            kinds += [("pa", c_), ("pb", c_), ("ga", c_), ("gb", c_)]

        def consumeA(i, wt, wb):
            kind, c_ = kinds[i]
            base = 4 * (c_ % 2)
            bank = base + {"pa": 0, "pb": 1, "ga": 2, "gb": 3}[kind]
            src, nk = {"pa": (ya4, AKC), "pb": (yb4, YBC), "ga": (xn4, KC), "gb": (xn4, KC)}[kind]
            w3 = wt[:, :nk * 128].rearrange("p (k c) -> p k c", c=128)
            for kc in range(nk):
                mm(PS[bank][:, :T], w3[:, kc, :], src[:, kc, :], kc == 0, kc == nk - 1, [wb, B_R1], [BPS[bank]])
            if kind == "gb":
                j = ec[0] % 2
                ec[0] += 1
                act(ea[j][:], PS[base + 2][:, :T], AF.Sigmoid, [BPS[base + 2], B_c], [B_ea[j]], bias=V("bga", c_))
                act(eb[j][:], PS[base + 3][:, :T], AF.Sigmoid, [BPS[base + 3], B_c], [B_eb[j]], bias=V("bgb", c_))
                tt(ea[j][:], ea[j][:], PS[base + 0][:, :T], ALU.mult, [B_ea[j], BPS[base + 0]], [B_ea[j]])
                tt(eb[j][:], eb[j][:], PS[base + 1][:, :T], ALU.mult, [B_eb[j], BPS[base + 1]], [B_eb[j]])
                tt(mixt[:, c_, :], ea[j][:], eb[j][:], ALU.add, [B_ea[j], B_eb[j]], [B_mix])

        gemm_stream(groups, consumeA, next_first=(w_o[0], KC * 128))
        Sc.barrier()
        for q4 in range(4):
            ksl = slice(q4 * KC // 4, (q4 + 1) * KC // 4)
            dma("sp", resid[:, ksl, :], xoT.rearrange("(kc p) t -> p kc t", p=128)[:, ksl, tsl], [], [B_R1])

        def consumeB(i, wt, wb):
            bank = 1 + (i % 2)
            w3 = wt[:, :KC * 128].rearrange("p (k c) -> p k c", c=128)
            for kc in range(KC):
                mm(PS[bank][:, :T], w3[:, kc, :], mixt[:, kc, :], kc == 0, kc == KC - 1, [wb, B_mix], [BPS[bank]])
            tt(resid[:, i, :], resid[:, i, :], PS[bank][:, :T], ALU.add, [B_R1, BPS[bank]], [B_R1])

        gemm_stream([(w_o[c_], KC * 128) for c_ in range(KC)], consumeB, next_first=(w_f1[0], KC * 128))
        rms_rstd(resid, B_R1)
        for kc in range(KC):
            stt(mixt[:, kc, :], resid[:, kc, :], V("nffn", kc), rst4[:], ALU.mult, ALU.mult, [B_R1, B_rst4, B_c], [B_mix])
        f0 = 0
        for pi, pn in enumerate(PARTS):
            groups, kinds = [], []
            for fl in range(pn):
                groups += [(w_f1[f0 + fl], KC * 128), (w_f3[f0 + fl], KC * 128)]
                kinds += [("f1", fl), ("f3", fl)]
            for c_ in range(KC):
                groups.append((w_f2[pi * KC + c_][:, :pn * 128], pn * 128))
                kinds.append(("f2", c_))

            def consumeC(i, wt, wb, kinds=kinds, pn=pn):
                kind, idx = kinds[i]
                if kind in ("f1", "f3"):
                    bank = 4 * (idx % 2) + (0 if kind == "f1" else 1)
                    w3 = wt[:, :KC * 128].rearrange("p (k c) -> p k c", c=128)
                    for kc in range(KC):
                        mm(PS[bank][:, :T], w3[:, kc, :], mixt[:, kc, :], kc == 0, kc == KC - 1, [wb, B_mix], [BPS[bank]])
                    if kind == "f3":
                        j = ec[0] % 2
                        ec[0] += 1
                        act(ea[j][:], PS[bank - 1][:, :T], AF.Silu, [BPS[bank - 1]], [B_ea[j]])
                        tt(hid[:, idx, :], ea[j][:], PS[bank][:, :T], ALU.mult, [B_ea[j], BPS[bank]], [B_hid])
                else:
                    bank = 2 + (idx % 2)
                    w3 = wt[:, :pn * 128].rearrange("p (k c) -> p k c", c=128)
                    for kc in range(pn):
                        mm(PS[bank][:, :T], w3[:, kc, :], hid[:, kc, :], kc == 0, kc == pn - 1, [wb, B_hid], [BPS[bank]])
                    tt(resid[:, idx, :], resid[:, idx, :], PS[bank][:, :T], ALU.add, [B_R1, BPS[bank]], [B_R1])

            f0 += pn
            gemm_stream(groups, consumeC, next_first=((w_f1[f0], KC * 128) if pi + 1 < len(PARTS) else (w_pg[0], KC * 128)))
        rms_rstd(resid, B_R1)
        for kc in range(KC):
            tt(mixt[:, kc, :], resid[:, kc, :], rst4[:], ALU.mult, [B_R1, B_rst4], [B_mix])
        dma("sp", ptf[:], pT.rearrange("(k p) t -> p k t", p=128)[:, :, tsl], [], [B_pt])
        cp(ptb[:], ptf[:], [B_pt], [B_pt])
        groups, kinds = [], []
        for c_ in range(KC):
            groups += [(w_pg[c_], KC * 128), (w_pl[c_], PC * 128)]
            kinds += [("pg", c_), ("pl", c_)]

        def consumeD(i, wt, wb, kinds=kinds):
            kind, c_ = kinds[i]
            bank = 4 * (c_ % 2) + (0 if kind == "pg" else 1)
            src, Bs, nk = (mixt, B_mix, KC) if kind == "pg" else (ptb, B_pt, PC)
            w3 = wt[:, :nk * 128].rearrange("p (k c) -> p k c", c=128)
            for kc in range(nk):
                mm(PS[bank][:, :T], w3[:, kc, :], src[:, kc, :], kc == 0, kc == nk - 1, [wb, Bs], [BPS[bank]])
            if kind == "pl":
                j = ec[0] % 2
                ec[0] += 1
                act(ea[j][:], PS[bank - 1][:, :T], AF.Sigmoid, [BPS[bank - 1]], [B_ea[j]])
                tt(ea[j][:], ea[j][:], PS[bank][:, :T], ALU.mult, [B_ea[j], BPS[bank]], [B_ea[j]])
                tt(resid[:, c_, :], resid[:, c_, :], ea[j][:], ALU.add, [B_R1, B_ea[j]], [B_R1])

        gemm_stream(groups, consumeD, next_first=((w_pa[0], AKC * 128) if m_ + 1 < NOWN else None))
        rms_rstd(resid, B_R1)
        for kc in range(KC):
            j = kc % 2
            stt(fo[j][:], resid[:, kc, :], V("nfin", kc), rst4[:], ALU.mult, ALU.mult, [B_R1, B_rst4, B_c], [B_fo[j]])
            dma("pool", outT[kc * 128:(kc + 1) * 128, tsl], fo[j][:], [B_fo[j]], [B_out], key=B_fo[j])
    Sc.emit(nc, es)
    es.close()
    return nc, dbg_out


def own_tokens(c, j):
    T = c["T"]
    return np.concatenate([np.arange((4 * m + j) * T, (4 * m + j + 1) * T) for m in range(c["NOWN"])])


def wlay(W, cw=128):
    K, N = W.shape
    kc, ng = K // 128, N // cw
    return np.ascontiguousarray(W.reshape(kc, 128, ng, cw).transpose(2, 1, 0, 3)).reshape(ng, 128, kc * cw)


def pvec(v):
    return np.ascontiguousarray(np.asarray(v, np.float32).reshape(-1, 128).T)


def prep_inputs(cfg, inp):
    c = derive(cfg)
    S, D, T, KC = c["S"], c["D"], c["T"], c["KC"]
    HA, G, HI, DB, DBc, NGB, DA = c["HA"], c["G"], c["HI"], c["DB"], c["DBc"], c["NGB"], c["DA"]
    PARTS, PMAX, FC = c["PARTS"], c["PMAX"], c["FC"]
    f = lambda a: np.asarray(a, np.float32)
    w_in = f(inp["w_in"])[0]
    o = 0
    sl = {}
    for n, sz in (("q", DA), ("k", G * 128), ("v", G * 128), ("qi", HI * 128), ("ki", 128), ("wi", HI),
                  ("rb", DB), ("kb", DB), ("vb", DB)):
        sl[n] = (o, o + sz)
        o += sz
    W = lambda n: w_in[:, sl[n][0]:sl[n][1]]
    shared = {}
    shared["w_kv"] = np.concatenate([wlay(W("k")), wlay(W("v"))], 0)
    shared["w_ki"] = wlay(W("ki"))
    shared["w_l1"] = wlay(f(inp["w1"])[0], c["LW"])
    shared["w_a1"] = wlay(f(inp["a1"])[0], c["LA"])
    shared["w_g1"] = wlay(f(inp["g1"])[0])
    shared["w_q"] = wlay(W("q"))
    shared["w_qi"] = wlay(W("qi"))
    shared["w_wi"] = wlay(W("wi"), HI)
    shared["w_pa"] = wlay(f(inp["w_pa"])[0])
    shared["w_pb"] = wlay(f(inp["w_pb"])[0])
    shared["w_g"] = wlay(f(inp["w_gate"])[0])
    shared["w_o"] = wlay(f(inp["w_o"])[0])
    shared["w_f1"] = wlay(f(inp["w_ffn1"])[0])
    shared["w_f3"] = wlay(f(inp["w_ffn3"])[0])
    w2 = f(inp["w_ffn2"])[0]
    wf2 = np.zeros((4 * KC, 128, PMAX * 128), np.float32)
    r0 = 0
    for pi, pn in enumerate(PARTS):
        blk = wlay(w2[r0 * 128:(r0 + pn) * 128, :])
        wf2[pi * KC:(pi + 1) * KC, :, :pn * 128] = blk
        r0 += pn
    shared["w_f2"] = wf2
    shared["w_pg"] = wlay(f(inp["w_ple_gate"])[0])
    shared["w_pl"] = wlay(f(inp["w_ple"])[0])
    shared["kpos"] = np.arange(S, dtype=np.float32)[None, :]
    clay, NCF = cf_layout()
    cfa = np.zeros((128, NCF), np.float32)
    r = np.arange(128)
    cfa[:, clay["ident"][0]:clay["ident"][0] + 128] = np.eye(128)
    cfa[:, clay["ones"][0]:clay["ones"][0] + 128] = 1.0
    cfa[:, clay["blk"][0]:clay["blk"][0] + 128] = (r[:, None] // 64 == r[None, :] // 64)
    cfa[:, clay["prot"][0]:clay["prot"][0] + 128] = (r[:, None] == (r[None, :] + 64) % 128)
    col = r[None, :]
    row = r[:, None] % 64
    cfa[:, clay["mk"][0]:clay["mk"][0] + 128] = np.where(col < 64, col > row, (col - 64) >= row)
    cfa[:, clay["mkl"][0]:clay["mkl"][0] + 64] = (row > np.arange(64)[None, :])
    cfa[:, clay["caus"][0]:clay["caus"][0] + 128] = np.where(r[None, :] <= r[:, None], 0.0, -1e30)
    cfa[:, clay["i64"][0]:clay["i64"][0] + 64] = (row == np.arange(64)[None, :])
    shared["cf"] = cfa
    vlay, NV = vec_layout(c)
    x = f(inp["x"])
    p = f(inp["p"])[0]
    posn = np.asarray(inp["positions"]).astype(np.int32)
    invf = (np.float32(10000.0) ** (-np.arange(0, 128, 2, dtype=np.float32) / np.float32(128))).astype(np.float32)
    xTb = [np.ascontiguousarray(x[b].T) for b in range(2)]
    maps = []
    for core in range(8):
        b, j = core // 4, core % 4
        own = own_tokens(c, j)
        ch = slice(j * DBc, (j + 1) * DBc)
        m = dict(shared)
        m["xT"] = xTb[b]
        m["xoT"] = np.ascontiguousarray(x[b][own].T)
        m["pos"] = np.ascontiguousarray(posn[b][None, :])
        m["poso"] = np.ascontiguousarray(posn[b][own][None, :])
        m["pT"] = np.ascontiguousarray(p[b][own].T)
        m["w_rkv"] = np.concatenate([wlay(W("rb")[:, ch]), wlay(W("kb")[:, ch]), wlay(W("vb")[:, ch])], 0)
        m["w2"] = np.ascontiguousarray(f(inp["w2"])[0][:, ch])
        m["a2"] = np.ascontiguousarray(f(inp["a2"])[0][:, ch])
        g2 = f(inp["g2"])[0][:, ch]
        m["g2"] = np.ascontiguousarray(g2.reshape(-1, 128, DBc).transpose(1, 0, 2)).reshape(128, -1)
        v = np.zeros((128, NV), np.float32)

        def put(name, arr):
            o_, w_ = vlay[name]
            v[:, o_:o_ + w_] = arr
        put("nmix", pvec(f(inp["norm_mix"])[0]))
        mw = f(inp["mu_wag"])[0]
        put("muw", pvec(mw[0])); put("mua", pvec(mw[1])); put("mug", pvec(mw[2]))
        put("nffn", pvec(f(inp["norm_ffn"])[0])); put("nfin", pvec(f(inp["norm_final"])))
        bg = f(inp["b_gate"])[0]
        put("bga", pvec(bg[:D])); put("bgb", pvec(bg[D:]))
        mr = f(inp["mu_rkv"])[0]
        put("mur", pvec(mr[0][ch])); put("muk", pvec(mr[1][ch])); put("muv", pvec(mr[2][ch]))
        put("w0", pvec(f(inp["w0"])[0][ch])); put("a0", pvec(f(inp["a0"])[0][ch]))
        put("kk", pvec(f(inp["k_k"])[0][ch])); put("ka", pvec(f(inp["k_a"])[0][ch]))
        put("rk", pvec(f(inp["r_k"])[0].reshape(-1)[ch]))
        put("lnw", pvec(f(inp["ln_w"])[0][ch])); put("lnb", pvec(f(inp["ln_b"])[0][ch]))
        put("invf", invf[np.arange(128) % 64][:, None])
        put("sgn", np.where(np.arange(128) < 64, -1.0, 1.0)[:, None])
        selv = np.zeros((128, 4), np.float32)
        selv[:, j] = 1.0
        put("sel", selv)
        put("qpos", own.astype(np.float32).reshape(-1, 128).T)
        m["vecs"] = v
        maps.append(m)
    return maps


_CACHE = {}


def run_cfg(cfg, inp, dbg=()):
    key = (tuple(sorted(cfg.items())), tuple(dbg))
    if key not in _CACHE:
        _CACHE[key] = build(cfg, dbg)
    nc, dbg_out = _CACHE[key]
    maps = prep_inputs(cfg, inp)
    res = run_bass_kernel_spmd(nc, maps, core_ids=list(range(8)))
    c = derive(cfg)
    out = np.zeros((2, c["S"], c["D"]), np.float32)
    for core in range(8):
        b, j = core // 4, core % 4
        out[b][own_tokens(c, j)] = np.asarray(res.results[core]["outT"]).T
    return out, res


def kernel(**inputs):
    out, _ = run_cfg(FULL, inputs)
    return out
```

```python
import math
from contextlib import ExitStack

import numpy as np
import ml_dtypes

import concourse.bass as bass
import concourse.mybir as mybir
from concourse.bass_utils import run_bass_kernel_spmd

F32 = mybir.dt.float32
BF16 = mybir.dt.bfloat16
I32 = mybir.dt.int32
AF = mybir.ActivationFunctionType
ALU = mybir.AluOpType

FULL = dict(B=2, S=8192, D=4096, HA=16, G=4, HI=16, DB=2048, LW=96, LA=96, LG=256,
            DFF=11008, DPLE=256, T=512, TOPK=256, NBIS=22)


class Buf:
    __slots__ = ("name", "w", "r", "dsem", "multi")

    def __init__(self, name="", multi=False):
        self.name = name
        self.w = {}
        self.r = []
        self.dsem = None
        self.multi = multi


class Op:
    __slots__ = ("fn", "waits", "inc")

    def __init__(self, fn, waits, inc):
        self.fn = fn
        self.waits = waits
        self.inc = inc


ENGS = ("pe", "act", "dve", "pool", "sp")


class Sched:
    def __init__(self):
        self.ops = {e: [] for e in ENGS}
        self.cnt = {}
        self.seen = {e: {} for e in ENGS}
        self.floor = {e: None for e in ENGS}
        self.ndsem = 0
        self.nops = 0

    def op(self, eng, fn, reads=(), writes=(), dma=False, key=None, amt=None, nofloor=False):
        self.nops += 1
        deps = {}

        def flat(lst):
            out = []
            for b in lst:
                if isinstance(b, (tuple, list)):
                    out.extend(flat(b))
                else:
                    out.append(b)
            return out
        reads = flat(reads)
        writes = flat(writes)

        def add(k, v):
            if deps.get(k, 0) < v:
                deps[k] = v

        if dma:
            kb = key if key is not None else (writes[0] if writes else reads[0])
            if kb.dsem is None:
                kb.dsem = ("d", self.ndsem)
                self.ndsem += 1
            mykey = kb.dsem
        else:
            mykey = ("e", eng)
        own = 0
        for b in reads:
            for k, v in b.w.items():
                add(k, v)
                if k == mykey:
                    own = max(own, v)
        for b in writes:
            if not b.multi:
                for k, v in b.w.items():
                    if dma and k == mykey:
                        continue
                    add(k, v)
                    if k == mykey:
                        own = max(own, v)
            for (k, v) in b.r:
                add(k, v)
        if not dma and mykey in deps:
            if eng == "pe" or own == 0:
                del deps[mykey]
            else:
                deps[mykey] = own
        if self.floor[eng] is not None and not nofloor:
            for k, v in self.floor[eng].items():
                if (k != mykey or dma) and deps.get(k, 0) < v:
                    deps[k] = v
            self.floor[eng] = None
        seen = self.seen[eng]
        waits = []
        for k, v in deps.items():
            if seen.get(k, 0) < v:
                seen[k] = v
                waits.append((k, v))
        inc = amt if amt is not None else (16 if dma else 1)
        self.cnt[mykey] = self.cnt.get(mykey, 0) + inc
        tok = (mykey, self.cnt[mykey])
        for b in reads:
            b.r.append(tok)
            if len(b.r) > 64:
                m = {}
                for k, v in b.r:
                    if m.get(k, 0) < v:
                        m[k] = v
                b.r = list(m.items())
        for b in writes:
            if b.multi:
                if b.w.get(tok[0], 0) < tok[1]:
                    b.w[tok[0]] = tok[1]
            else:
                b.w = {tok[0]: tok[1]}
            b.r = []
        self.ops[eng].append(Op(fn, waits, (mykey, inc)))
        return tok

    def capture(self, fn):
        cap = []
        real = self.op
        self.op = lambda *a, **k: cap.append((a, k))
        try:
            fn()
        finally:
            self.op = real
        return cap

    def replay_interleaved(self, la, lb):
        ia = ib = 0
        na, nb = len(la), len(lb)
        while ia < na or ib < nb:
            if ib >= nb or (ia < na and ia * nb <= ib * na):
                a, k = la[ia]
                ia += 1
            else:
                a, k = lb[ib]
                ib += 1
            self.op(*a, **k)

    def barrier(self):
        snap = dict(self.cnt)
        for e in ENGS:
            self.floor[e] = dict(snap)

    def emit(self, nc, es, final_eng="sp"):
        self.barrier()
        fl = self.floor[final_eng]
        waits = [(k, v) for k, v in fl.items() if self.seen[final_eng].get(k, 0) < v]
        self.ops[final_eng].append(Op(None, waits, None))
        sems = {}
        for k in list(self.cnt.keys()):
            sems[k] = es.enter_context(nc.semaphore("s%s%s" % (k[0], k[1])))
        block = es.enter_context(nc.Block())
        engmap = {"pe": block.tensor, "act": block.scalar, "dve": block.vector,
                  "pool": block.gpsimd, "sp": block.sync}

        def make(ename):
            ops = self.ops[ename]

            def body(eng):
                for o in ops:
                    for k, v in o.waits:
                        eng.wait_ge(sems[k], v)
                    if o.fn is not None:
                        ins = o.fn(eng)
                        ins.then_inc(sems[o.inc[0]], o.inc[1])
            return body

        for ename in ENGS:
            engmap[ename](make(ename))


def derive(cfg):
    c = dict(cfg)
    c["KC"] = c["D"] // 128
    c["NT"] = c["S"] // c["T"]
    c["NOWN"] = c["NT"] // 4
    c["NTOK"] = c["NOWN"] * c["T"]
    c["NQB"] = c["NTOK"] // 128
    c["DBc"] = c["DB"] // 4
    c["NGB"] = c["DBc"] // 128
    c["NCH"] = c["T"] // 64
    c["FC"] = c["DFF"] // 128
    NP = 4
    base, rem = divmod(c["FC"], NP)
    c["PARTS"] = [base + (1 if i < rem else 0) for i in range(NP)]
    c["PMAX"] = max(c["PARTS"])
    c["HPG"] = c["HA"] // c["G"]
    c["DA"] = c["HA"] * 128
    c["YBC"] = c["DB"] // 128
    return c


def vec_layout(c):
    KC, NGB, NQB = c["KC"], c["NGB"], c["NQB"]
    items = [("nmix", KC), ("muw", KC), ("mua", KC), ("mug", KC), ("nffn", KC), ("nfin", KC),
             ("bga", KC), ("bgb", KC)]
    for n in ("mur", "muk", "muv", "w0", "a0", "kk", "ka", "rk", "lnw", "lnb"):
        items.append((n, NGB))
    items += [("invf", 1), ("sgn", 1), ("sel", 4), ("qpos", NQB)]
    lay = {}
    off = 0
    for n, w in items:
        lay[n] = (off, w)
        off += w
    return lay, off


CF_ITEMS = [("ident", 128), ("ones", 128), ("blk", 128), ("prot", 128), ("mk", 128), ("mkl", 64),
            ("caus", 128), ("i64", 64)]


def cf_layout():
    lay = {}
    off = 0
    for n, w in CF_ITEMS:
        lay[n] = (off, w)
        off += w
    return lay, off


def build(cfg, dbg=()):
    c = derive(cfg)
    S, D, KC, T, NT, NOWN, NTOK, NQB = c["S"], c["D"], c["KC"], c["T"], c["NT"], c["NOWN"], c["NTOK"], c["NQB"]
    HA, G, HI, HPG, DA = c["HA"], c["G"], c["HI"], c["HPG"], c["DA"]
    DBc, NGB, NCH, FC, PARTS, PMAX = c["DBc"], c["NGB"], c["NCH"], c["FC"], c["PARTS"], c["PMAX"]
    LW, LA, LG, DPLE, TOPK, NBIS, YBC = c["LW"], c["LA"], c["LG"], c["DPLE"], c["TOPK"], c["NBIS"], c["YBC"]
    LGC = LG // 128
    PC = DPLE // 128
    AKC = DA // 128
    vlay, NV = vec_layout(c)
    clay, NCF = cf_layout()

    nc = bass.Bass("TRN2", target_bir_lowering=False)
    Sc = Sched()
    es = ExitStack()

    def din(name, shape, dt=F32):
        return nc.dram_tensor(name, list(shape), dt, kind="ExternalInput").ap()

    dbg_out = {}

    def dscr(name, shape, dt):
        if name in dbg:
            t = nc.dram_tensor(name, list(shape), dt, kind="ExternalOutput")
            dbg_out[name] = t
        else:
            t = nc.dram_tensor(name, list(shape), dt)
        return t

    xT = din("xT", [D, S])
    xoT = din("xoT", [D, NTOK])
    pos = din("pos", [1, S], I32)
    poso = din("poso", [1, NTOK], I32)
    pT = din("pT", [DPLE, NTOK])
    vecs_d = din("vecs", [128, NV])
    cf_d = din("cf", [128, NCF])
    kpos_d = din("kpos", [1, S])
    w_kv = din("w_kv", [2 * G, 128, KC * 128])
    w_ki = din("w_ki", [1, 128, KC * 128])
    w_rkv = din("w_rkv", [3 * NGB, 128, KC * 128])
    w_l1 = din("w_l1", [1, 128, KC * LW])
    w_a1 = din("w_a1", [1, 128, KC * LA])
    w_g1 = din("w_g1", [LGC, 128, KC * 128])
    w2_d = din("w2", [LW, DBc])
    a2_d = din("a2", [LA, DBc])
    g2_d = din("g2", [128, LGC * DBc])
    w_q = din("w_q", [HA, 128, KC * 128])
    w_qi = din("w_qi", [HI, 128, KC * 128])
    w_wi = din("w_wi", [1, 128, KC * HI])
    w_pa = din("w_pa", [KC, 128, AKC * 128])
    w_pb = din("w_pb", [KC, 128, YBC * 128])
    w_g = din("w_g", [2 * KC, 128, KC * 128])
    w_o = din("w_o", [KC, 128, KC * 128])
    w_f1 = din("w_f1", [FC, 128, KC * 128])
    w_f3 = din("w_f3", [FC, 128, KC * 128])
    w_f2 = din("w_f2", [4 * KC, 128, PMAX * 128])
    w_pg = din("w_pg", [KC, 128, KC * 128])
    w_pl = din("w_pl", [KC, 128, PC * 128])
    outT = nc.dram_tensor("outT", [D, NTOK], F32, kind="ExternalOutput").ap()

    KT_t = dscr("KT", [G, 128, S], BF16)
    VT_t = dscr("Vtok", [S, G * 128], BF16)
    kiT_t = dscr("kiT", [128, S], BF16)
    rr_t = dscr("rr", [3 * NGB, 128, S], F32)
    wl_t = dscr("wl", [3 * NGB, 128, S], F32)
    qT_t = dscr("qT", [HA, 128, NTOK], BF16)
    qiT_t = dscr("qiT", [HI, 128, NTOK], BF16)
    wi_t = dscr("wi", [NTOK, HI], F32)
    xno_t = dscr("xno", [KC, 128, NTOK], BF16)
    yaT_t = dscr("yaT", [HA, 128, NTOK], BF16)
    NLG = 2 + LGC
    NGA1, NGO1 = 2 * G + 1 + 3 * NGB, HA + HI + 1
    wca = nc.dram_tensor("wca", [NGA1, 128, KC * 128], BF16).ap()
    wco = nc.dram_tensor("wco", [NGO1, 128, KC * 128], BF16).ap()
    B_wca, B_wco = Buf("wca", multi=True), Buf("wco", multi=True)
    wlp_t = nc.dram_tensor("wlp", [NLG, 128, 2 * KC * 128], BF16)
    wlp = wlp_t.ap()
    B_wlp = Buf("wlp", multi=True)
    ybs_t = [dscr("ybs%d" % n, [NGB * 128, T], BF16) for n in range(NT)]
    yba_t = [nc.dram_tensor("yba%d" % n, [4 * NGB * 128, T], BF16) for n in range(NT)]
    KT, VT, kiT, rr, wl, qT, qiT, wi_d, xno, yaT = [t.ap() for t in (
        KT_t, VT_t, kiT_t, rr_t, wl_t, qT_t, qiT_t, wi_t, xno_t, yaT_t)]
    ybs3 = [t.ap().rearrange("(g p) t -> g p t", g=NGB) for t in ybs_t]
    yba4 = [t.ap().rearrange("(r g p) t -> r g p t", r=4, g=NGB) for t in yba_t]
    B_KT, B_VT, B_kiT, B_rr, B_wl, B_q, B_wi, B_xno, B_ya, B_ybs, B_yba, B_out = [Buf(n, multi=True) for n in (
        "KT", "VT", "kiT", "rr", "wl", "q", "wi", "xno", "ya", "ybs", "yba", "out")]

    sbo = [16512]
    uid = [0]

    def sb(shape, dt, name="t"):
        esz = 4 if dt in (F32, I32) else 2
        n = 1
        for s_ in shape[1:]:
            n *= s_
        nbytes = (n * esz + 63) // 64 * 64
        uid[0] += 1
        t = nc.alloc_sbuf_tensor_at("%s_%d" % (name, uid[0]), list(shape), dt, offset=sbo[0])
        sbo[0] += nbytes
        assert sbo[0] <= 229344, ("SBUF overflow", name, sbo[0])
        return t

    PS = [es.enter_context(nc.psum_tensor("ps%d" % i, [128, 512], F32)) for i in range(8)]
    BPS = [Buf("ps%d" % i) for i in range(8)]

    def mm(out, lhsT, rhs, start, stop, R, W):
        Sc.op("pe", lambda e: e.matmul(out, lhsT=lhsT, rhs=rhs, start=start, stop=stop), reads=R, writes=W)

    def tr(out, in_, ident, R, W):
        Sc.op("pe", lambda e: e.transpose(out, in_, ident), reads=R, writes=W)

    def act(out, in_, func, R, W, bias=None, scale=None, accum=None):
        kw = {}
        if bias is not None:
            kw["bias"] = bias
        if scale is not None:
            kw["scale"] = scale
        if accum is not None:
            kw["accum_out"] = accum
        Sc.op("act", lambda e: e.activation(out, in_, func, **kw), reads=R, writes=W)

    def tt(out, in0, in1, op, R, W, eng="dve"):
        Sc.op(eng, lambda e: e.tensor_tensor(out, in0, in1, op), reads=R, writes=W)

    def ts(out, in0, s1, op0, R, W, s2=None, op1=None, accum=None, eng="dve"):
        kw = {}
        if op1 is not None:
            kw["op1"] = op1
        if accum is not None:
            kw["accum_out"] = accum
        Sc.op(eng, lambda e: e.tensor_scalar(out, in0, s1, s2, op0, **kw), reads=R, writes=W)

    def stt(out, in0, scalar, in1, op0, op1, R, W):
        Sc.op("dve", lambda e: e.scalar_tensor_tensor(out, in0, scalar, in1, op0, op1), reads=R, writes=W)

    def cp(out, in_, R, W, eng="dve"):
        Sc.op(eng, lambda e: e.tensor_copy(out, in_), reads=R, writes=W)

    def recip(out, in_, R, W):
        Sc.op("dve", lambda e: e.reciprocal(out, in_), reads=R, writes=W)

    def memset(ap, val, W, eng="dve"):
        Sc.op(eng, lambda e: e.memset(ap, val), writes=W)

    def dma(q, out, in_, R, W, key=None):
        Sc.op(q, lambda e: e.dma_start(out=out, in_=in_), reads=R, writes=W, dma=True, key=key)

    vecs = sb([128, NV], F32, "vecs")
    cf = sb([128, NCF], F32, "cf")
    cb = sb([128, NCF], BF16, "cb")
    omu = sb([128, 3 * KC], F32, "omu")
    omur = sb([128, 3 * NGB], F32, "omur")
    B_c = Buf("consts")
    dma("sp", vecs[:], vecs_d, [], [B_c])
    B_cf = Buf("cf")
    dma("sp", cf[:], cf_d, [], [B_cf])
    cp(cb[:], cf[:], [B_cf], [B_c])

    def V(name, i=None):
        o, w = vlay[name]
        if i is None:
            return vecs[:, o:o + w]
        return vecs[:, o + i:o + i + 1]

    def CF(name, rows=slice(0, 128)):
        o, w = clay[name]
        return cf[rows, o:o + w]

    def CB(name, rows=slice(0, 128)):
        o, w = clay[name]
        return cb[rows, o:o + w]

    o_, w_ = vlay["muw"]
    ts(omu[:], vecs[:, o_:o_ + 3 * KC], -1.0, ALU.mult, [B_c], [B_c], s2=1.0, op1=ALU.add)
    o_, w_ = vlay["mur"]
    ts(omur[:], vecs[:, o_:o_ + 3 * NGB], -1.0, ALU.mult, [B_c], [B_c], s2=1.0, op1=ALU.add)

    WMAXE = max(KC, AKC, YBC, PMAX) * 128
    NST, NWB = 2, 2
    wst_off = []
    wst = []
    for _ in range(NST):
        wst_off.append(sbo[0])
        wst.append(sb([128, WMAXE], F32, "wst"))
    wstbf = [nc.alloc_sbuf_tensor_at("wstbf%d" % i, [128, WMAXE], BF16, offset=wst_off[i]) for i in range(NST)]
    wbf = [sb([128, WMAXE], BF16, "wbf") for _ in range(NWB)]
    B_wst = [Buf("wst%d" % i) for i in range(NST)]
    B_wbf = [tuple(Buf("wbf%d_%d" % (i, k)) for k in range(3)) for i in range(NWB)]
    wctr = [0, 0]
    CAST_FR = (0.45, 0.35)

    def wload(src, n):
        i = wctr[0] % NST
        wctr[0] += 1
        j = wctr[1] % NWB
        wctr[1] += 1
        Sc.op("sp", lambda e, o_=wst[i][:, :n], i_=src: e.dma_start(out=o_, in_=i_), reads=[], writes=[B_wst[i]],
              dma=True, nofloor=True)
        c1 = int(n * CAST_FR[0]) // 64 * 64
        c2 = c1 + int(n * CAST_FR[1]) // 64 * 64
        if c1 > 0:
            act(wbf[j][:, :c1], wst[i][:, :c1], AF.Copy, [B_wst[i]], [B_wbf[j][0]])
        if c2 > c1:
            cp(wbf[j][:, c1:c2], wst[i][:, c1:c2], [B_wst[i]], [B_wbf[j][1]])
        if n > c2:
            cp(wbf[j][:, c2:n], wst[i][:, c2:n], [B_wst[i]], [B_wbf[j][2]], eng="pool")
        return wbf[j], B_wbf[j]

    dctr = [0]

    def wload_bf(src, n, rb=None, deep=False):
        if deep:
            DS = [(wstbf[0], (B_wst[0],)), (wstbf[1], (B_wst[1],)), (wbf[0], B_wbf[0]), (wbf[1], B_wbf[1])]
            tl, bufs = DS[dctr[0] % 4]
            dctr[0] += 1
        else:
            j = wctr[1] % NWB
            wctr[1] += 1
            tl, bufs = wbf[j], B_wbf[j]
        Sc.op("sp", lambda e, o_=tl[:, :n], i_=src: e.dma_start(out=o_, in_=i_),
              reads=[rb if rb is not None else B_wlp], writes=list(bufs), dma=True, key=bufs[0], nofloor=True)
        return tl, bufs

    prefetched = [None]
    last_deep = [False]

    def gemm_stream(groups, consume, next_first=None):
        deep = all(len(g) > 2 for g in groups) and (next_first is None or len(next_first) > 2)
        if deep and not last_deep[0]:
            dctr[0] = 0
        last_deep[0] = deep

        def ld(g):
            if len(g) > 2:
                return wload_bf(g[0], g[1], g[3] if len(g) > 3 else None, deep=deep)
            return wload(g[0], g[1])
        if prefetched[0] is not None:
            assert prefetched[0][2] == groups[0][1], (prefetched[0][2], groups[0][1])
            pend = prefetched[0][:2]
            prefetched[0] = None
        else:
            pend = ld(groups[0])
        for i in range(len(groups)):
            if i + 1 < len(groups):
                nxt = ld(groups[i + 1])
            elif next_first is not None:
                t_, b_ = ld(next_first)
                prefetched[0] = (t_, b_, next_first[1])
                nxt = None
            else:
                nxt = None
            consume(i, pend[0], pend[1])
            pend = nxt

    g_mark = sbo[0]

    def rmsnorm(xt, B_xt, outt, B_out, gname, sq, B_sq, rst, B_rst, bank, width=None, off=0):
        for kc in range(KC):
            i = kc % 2
            act(sq[i][:], xt[:, kc, :], AF.Square, [B_xt], [B_sq[i]])
            mm(PS[bank][:, :T], CF("ones"), sq[i][:], kc == 0, kc == KC - 1, [B_sq[i], B_cf], [BPS[bank]])
        act(rst[:], PS[bank][:, :T], AF.Sqrt, [BPS[bank]], [B_rst], bias=EPS_AP[:], scale=1.0 / D)
        recip(rst[:], rst[:], [B_rst], [B_rst])
        for kc in range(KC):
            if gname is not None:
                stt(outt[:, kc, off:off + T], xt[:, kc, :], V(gname, kc), rst[:], ALU.mult, ALU.mult,
                    [B_xt, B_rst, B_c], [B_out])
            else:
                tt(outt[:, kc, off:off + T], xt[:, kc, :], rst[:], ALU.mult, [B_xt, B_rst], [B_out])

    eps_t = sb([128, 2], F32, "eps")
    EPS_AP = eps_t[:, 0:1]
    memset(eps_t[:, 0:1], 1e-6, [B_c])
    memset(eps_t[:, 1:2], 64e-5, [B_c])
    GNEPS_AP = eps_t[:, 1:2]
    g_mark = sbo[0]

    TWO_PI = 6.283185

    def rope_tables(pos_ap, cosT, sinT, B_tab, tmp):
        pi_, u, ui, f1, f2 = tmp
        B_t = Buf("ropetmp")
        dma("sp", pi_[:], pos_ap.to_broadcast([128, T]), [], [B_t])
        ts(u[:], pi_[:], V("invf"), ALU.mult, [B_t, B_c], [B_t], s2=1.0 / (2 * math.pi), op1=ALU.mult)
        for which in (0, 1):
            if which == 1:
                ts(u[:], u[:], 0.25, ALU.add, [B_t], [B_t])
            cp(ui[:], u[:], [B_t], [B_t])
            tt(f1[:], u[:], ui[:], ALU.subtract, [B_t], [B_t])
            stt(f2[:], f1[:], 0.5, f1[:], ALU.is_gt, ALU.subtract, [B_t], [B_t])
            stt(f1[:], f2[:], 0.5, f2[:], ALU.is_gt, ALU.subtract, [B_t], [B_t])
            if which == 0:
                ts(f1[:], f1[:], V("sgn"), ALU.mult, [B_t, B_c], [B_t])
                act(sinT[:], f1[:], AF.Sin, [B_t], [B_tab], scale=TWO_PI)
            else:
                act(cosT[:], f1[:], AF.Sin, [B_t], [B_tab], scale=TWO_PI)

    def rope_epilogue(bank, cosT, sinT, B_tab, tq, tc, B_e, out_bf, B_o, rbank):
        act(tq[:], PS[bank][:, :T], AF.Copy, [BPS[bank]], [B_e])
        tt(tc[:], PS[bank][:, :T], cosT[:], ALU.mult, [BPS[bank], B_tab], [B_e])
        mm(PS[rbank][:, :T], CB("prot"), tq[:], True, True, [B_e, B_c], [BPS[rbank]])
        tt(tq_f[:], PS[rbank][:, :T], sinT[:], ALU.mult, [BPS[rbank], B_tab], [B_e2])
        tt(out_bf, tq_f[:], tc[:], ALU.add, [B_e2, B_e], [B_o])

    xt_off = sbo[0]
    xt = sb([128, KC, T], F32, "xt")
    B_xt = Buf("xt")
    lsc = nc.alloc_sbuf_tensor_at("lsc", [128, 2, KC * 128], BF16, offset=xt_off)
    assert 2 * KC * 128 * 2 <= KC * T * 4
    lgs = [(w_l1[0], LW, 0, "muw"), (w_a1[0], LA, KC, "mua")] + [(w_g1[k], 128, 2 * KC, "mug") for k in range(LGC)]
    for gi, (srcw, cw, mo, mun) in enumerate(lgs):
        i_ = wctr[0] % NST
        wctr[0] += 1
        dma("sp", wst[i_][:, :KC * cw], srcw, [], [B_wst[i_]])
        for kc in range(KC):
            eng_ = "dve" if kc % 2 == 0 else "pool"
            ts(lsc[:, 0, kc * cw:(kc + 1) * cw], wst[i_][:, kc * cw:(kc + 1) * cw], omu[:, mo + kc:mo + kc + 1], ALU.mult,
               [B_wst[i_], B_c], [B_xt], eng=eng_)
            ts(lsc[:, 1, kc * cw:(kc + 1) * cw], wst[i_][:, kc * cw:(kc + 1) * cw], V(mun, kc), ALU.mult,
               [B_wst[i_], B_c], [B_xt], eng=eng_)
        for v_ in range(2):
            dma("pool", wlp[gi, :, v_ * KC * 128:v_ * KC * 128 + KC * cw], lsc[:, v_, :KC * cw], [B_xt], [B_wlp], key=B_xt)
    xn = sb([128, KC, T + 1], BF16, "xn")
    B_xn = Buf("xn")
    sq = [sb([128, T], F32, "sq") for _ in range(2)]
    B_sq = [Buf("sq0"), Buf("sq1")]
    rst = sb([128, T], F32, "rst")
    B_rst = Buf("rst")
    cosT = sb([128, T], F32, "cos")
    sinT = sb([128, T], F32, "sin")
    B_tab = Buf("tab")
    rtmp = [sb([128, T], I32, "pi"), sb([128, T], F32, "u"), sb([128, T], I32, "ui"),
            sb([128, T], F32, "f1"), sb([128, T], F32, "f2")]
    tq = sb([128, T], BF16, "tq")
    tc = sb([128, T], F32, "tc")
    tq_f = sb([128, T], F32, "tqf")
    B_e, B_e2 = Buf("e"), Buf("e2")
    ob = [sb([128, T], BF16, "ob") for _ in range(2)]
    B_ob = [Buf("ob0"), Buf("ob1")]
    of = [sb([128, T], F32, "of") for _ in range(2)]
    B_of = [Buf("of0"), Buf("of1")]
    v4 = sb([128, G, T], BF16, "v4")
    B_v4 = Buf("v4")
    vtk = sb([128, G * 128], BF16, "vtk")
    B_vtk = Buf("vtk")
    xl = [sb([128, T], BF16, "xl") for _ in range(3)]
    B_xl = [Buf("xl%d" % i) for i in range(3)]
    xlt = [sb([128, T], F32, "xlt") for _ in range(2)]
    B_xlt = [Buf("xlt0"), Buf("xlt1")]
    hl = sb([128, LGC, T], BF16, "hl")
    B_hl = Buf("hl")
    w2s = sb([128, DBc], F32, "w2s")
    w2b = sb([128, 2 + LGC, DBc], BF16, "w2b")
    B_w2 = Buf("w2")
    dma("sp", w2s[:LW, :], w2_d, [], [B_w2])
    cp(w2b[:LW, 0, :], w2s[:LW, :], [B_w2], [B_w2])
    dma("sp", w2s[:LA, :], a2_d, [], [B_w2])
    cp(w2b[:LA, 1, :], w2s[:LA, :], [B_w2], [B_w2])
    for k in range(LGC):
        dma("sp", w2s[:, :], g2_d[:, k * DBc:(k + 1) * DBc], [], [B_w2])
        cp(w2b[:, 2 + k, :], w2s[:, :], [B_w2], [B_w2])
    octr = [0]

    s1_seq = [(n, False) for n in range(NT)] + [(n, True) for n in range(NOWN)]
    x_loaded = set()

    def load_x(si):
        if si in x_loaded or si >= len(s1_seq):
            return
        x_loaded.add(si)
        n, own = s1_seq[si]
        src = xoT if own else xT
        for q4 in range(4):
            ksl = slice(q4 * KC // 4, (q4 + 1) * KC // 4)
            dma("sp", xt[:, ksl, :], src.rearrange("(kc p) t -> p kc t", p=128)[:, ksl, n * T:(n + 1) * T], [], [B_xt])

    stats_done = set()

    def rms_stats(si):
        if si in stats_done or si >= len(s1_seq):
            return
        stats_done.add(si)
        load_x(si)
        for kc in range(KC):
            i = kc % 2
            act(sq[i][:], xt[:, kc, :], AF.Square, [B_xt], [B_sq[i]])
            mm(PS[0][:, :T], CF("ones"), sq[i][:], kc == 0, kc == KC - 1, [B_sq[i], B_cf], [BPS[0]])
        act(rst[:], PS[0][:, :T], AF.Sqrt, [BPS[0]], [B_rst], bias=EPS_AP[:], scale=1.0 / D)
        recip(rst[:], rst[:], [B_rst], [B_rst])

    def s1_tile(si_):
        n, own = s1_seq[si_]
        psrc = poso if own else pos
        load_x(si_)
        if n == 0 or own:
            memset(xn[:, :, 0:1], 0.0, [B_xn])
        else:
            cp(xn[:, :, 0:1], xn[:, :, T:T + 1], [B_xn], [B_xn])
        rms_stats(si_)
        for kc in range(KC):
            stt(xn[:, kc, 1:T + 1], xt[:, kc, :], V("nmix", kc), rst[:], ALU.mult, ALU.mult, [B_xt, B_rst, B_c], [B_xn])
        rope_tables(psrc[:, n * T:(n + 1) * T], cosT, sinT, B_tab, rtmp)
        if own:
            for kc in range(KC):
                dma("pool", xno[kc, :, n * T:(n + 1) * T], xn[:, kc, 1:T + 1], [B_xn], [B_xno], key=B_xn)
        groups = []
        kinds = []
        if own:
            for h in range(HA):
                groups.append((w_q[h], KC * 128)); kinds.append(("q", h))
            for h in range(HI):
                groups.append((w_qi[h], KC * 128)); kinds.append(("qi", h))
            groups.append((w_wi[0], KC * HI)); kinds.append(("wi", 0))
        else:
            for g in range(G):
                groups.append((w_kv[g], KC * 128)); kinds.append(("k", g))
            groups.append((w_ki[0], KC * 128)); kinds.append(("ki", 0))
            for g in range(G):
                groups.append((w_kv[G + g], KC * 128)); kinds.append(("v", g))
            for i in range(3 * NGB):
                groups.append((w_rkv[i], KC * 128)); kinds.append(("rkv", i))
            for gi_, (kd, idx_, cw_) in enumerate([("l1", 0, LW), ("a1", 0, LA)] + [("g1", k, 128) for k in range(LGC)]):
                for v_ in range(2):
                    groups.append((wlp[gi_, :, v_ * KC * 128:v_ * KC * 128 + KC * cw_], KC * cw_, "bf"))
                    kinds.append((kd, idx_ * 2 + v_))

        wc_, B_wc_ = (wco, B_wco) if own else (wca, B_wca)
        cache_n = [g_[1] for g_ in groups if len(g_) == 2]
        if n >= 1:
            groups = [((wc_[ci_][:, :g_[1]], g_[1], "bf", B_wc_) if len(g_) == 2 else g_) for ci_, g_ in enumerate(groups)]

        def consume(i, wt, wb):
            kind, idx = kinds[i]
            bank = 1 + (i % 2)
            t0, t1 = n * T, (n + 1) * T
            if n == 0 and i < len(cache_n):
                ne_ = cache_n[i]
                dma("pool", wc_[i][:, :ne_], wt[:, :ne_], [wb], [B_wc_], key=wb[0])
            if i == 2:
                load_x(si_ + 1)
            if i == 12:
                rms_stats(si_ + 1)
            if kind == "wi":
                w3 = wt[:, :KC * HI].rearrange("p (k c) -> p k c", c=HI)
                for tb in range(T // 128):
                    for kc in range(KC):
                        mm(PS[bank][:, tb * HI:(tb + 1) * HI], xn[:, kc, 1 + tb * 128:1 + (tb + 1) * 128], w3[:, kc, :],
                           kc == 0, kc == KC - 1, [B_xn, wb], [BPS[bank]])
                j = octr[0] % 2
                octr[0] += 1
                ts(of[j][:, :(T // 128) * HI], PS[bank][:, :(T // 128) * HI], float(HI ** -0.5 * 128 ** -0.5), ALU.mult,
                   [BPS[bank]], [B_of[j]])
                for tb in range(T // 128):
                    dma("pool", wi_d[n * T + tb * 128:n * T + (tb + 1) * 128, :], of[j][:, tb * HI:(tb + 1) * HI],
                        [B_of[j]], [B_wi], key=B_of[j])
                return
            if kind in ("l1", "a1", "g1"):
                cw = {"l1": LW, "a1": LA, "g1": 128}[kind]
                ver = idx % 2
                idx = idx // 2
                bank = 5
                w3 = wt[:, :KC * cw].rearrange("p (k c) -> p k c", c=cw)
                for kc in range(KC):
                    rhs_ = xn[:, kc, 1:T + 1] if ver == 0 else xn[:, kc, 0:T]
                    mm(PS[bank][:cw, :T], w3[:, kc, :], rhs_, ver == 0 and kc == 0, ver == 1 and kc == KC - 1,
                       [wb, B_xn], [BPS[bank]])
                if ver == 0:
                    return
                fn = {"l1": AF.Tanh, "a1": AF.Copy, "g1": AF.Sigmoid}[kind]
                act(hl[:cw, idx if kind == "g1" else 0, :], PS[bank][:cw, :T], fn, [BPS[bank]], [B_hl])
                if kind == "g1" and idx < LGC - 1:
                    return
                slot = {"l1": 0, "a1": 1, "g1": 2}[kind]
                for gb in range(NGB):
                    if kind == "g1":
                        for k in range(LGC):
                            mm(PS[3][:, :T], w2b[:, 2 + k, gb * 128:(gb + 1) * 128], hl[:, k, :], k == 0, k == LGC - 1,
                               [B_w2, B_hl], [BPS[3]])
                    else:
                        mm(PS[3][:, :T], w2b[:cw, slot, gb * 128:(gb + 1) * 128], hl[:cw, 0, :], True, True,
                           [B_w2, B_hl], [BPS[3]])
                    j = octr[0] % 2
                    octr[0] += 1
                    act(of[j][:], PS[3][:, :T], AF.Copy, [BPS[3]], [B_of[j]])
                    dma("pool", wl[slot * NGB + gb, :, t0:t1], of[j][:], [B_of[j]], [B_wl], key=B_of[j])
                return
            w3 = wt[:, :KC * 128].rearrange("p (k c) -> p k c", c=128)
            for kc in range(KC):
                mm(PS[bank][:, :T], w3[:, kc, :], xn[:, kc, 1:T + 1], kc == 0, kc == KC - 1, [wb, B_xn], [BPS[bank]])
            if kind in ("k", "ki", "q", "qi"):
                j = octr[0] % 2
                octr[0] += 1
                rope_epilogue(bank, cosT, sinT, B_tab, tq, tc, B_e, ob[j][:], B_ob[j], 3)
                if kind == "k":
                    dma("pool", KT[idx, :, t0:t1], ob[j][:], [B_ob[j]], [B_KT], key=B_ob[j])
                elif kind == "ki":
                    dma("pool", kiT[:, t0:t1], ob[j][:], [B_ob[j]], [B_kiT], key=B_ob[j])
                elif kind == "q":
                    dma("pool", qT[idx, :, t0:t1], ob[j][:], [B_ob[j]], [B_q], key=B_ob[j])
                else:
                    dma("pool", qiT[idx, :, t0:t1], ob[j][:], [B_ob[j]], [B_q], key=B_ob[j])
            elif kind == "v":
                act(v4[:, idx, :], PS[bank][:, :T], AF.Copy, [BPS[bank]], [B_v4])
                if idx == G - 1:
                    for tb in range(T // 128):
                        pv = PS[4][:].bitcast(BF16)
                        for g in range(G):
                            tr(pv[:, g * 128:(g + 1) * 128], v4[:, g, tb * 128:(tb + 1) * 128], CB("ident"),
                               [B_v4, B_c], [BPS[4]])
                        cp(vtk[:], pv[:, :G * 128], [BPS[4]], [B_vtk])
                        dma("pool", VT[t0 + tb * 128:t0 + (tb + 1) * 128, :], vtk[:], [B_vtk], [B_VT], key=B_vtk)
            elif kind == "rkv":
                j = octr[0] % 2
                octr[0] += 1
                act(of[j][:], PS[bank][:, :T], AF.Copy, [BPS[bank]], [B_of[j]])
                dma("pool", rr[idx, :, t0:t1], of[j][:], [B_of[j]], [B_rr], key=B_of[j])

        nf = None
        if si_ + 1 < len(s1_seq):
            n2_, own2_ = s1_seq[si_ + 1]
            if n2_ >= 1:
                nf = (wco[0], KC * 128, "bf", B_wco) if own2_ else (wca[0], KC * 128, "bf", B_wca)
            elif n == 0:
                nf = (w_q[0], KC * 128) if own2_ else (w_kv[0], KC * 128)
        gemm_stream(groups, consume, next_first=nf)

    for si_ in range(len(s1_seq)):
        s1_tile(si_)

    Sc.barrier()
    sbo[0] = g_mark


    Sc.barrier()
    sbo[0] = g_mark
    NLEV = 5
    B_cc = Buf("cc")
    HS = [slice(0, 64), slice(64, 128)]

    def t3(t):
        return t[:].rearrange("p (c t) -> p c t", t=64)

    onesT = sb([128, T], F32, "onesT")
    memset(onesT[:], 1.0, [B_c])
    Rr = [sb([128, T + 1], F32, "Rr") for _ in range(3)]
    B_Rr = [Buf("Rr%d" % i) for i in range(3)]
    Ll = [sb([128, T], F32, "Ll") for _ in range(3)]
    B_Ll = [Buf("Ll%d" % i) for i in range(3)]
    NTMP = 10
    tm = [sb([128, T], F32, "tm") for _ in range(NTMP)]
    B_tm = [Buf("tm%d" % i) for i in range(NTMP)]
    vb16_2 = [sb([128, T], BF16, "vb16") for _ in range(2)]
    Bp_2 = [sb([128, T], BF16, "Bp") for _ in range(2)]
    Kp_2 = [sb([128, T], BF16, "Kp") for _ in range(2)]
    B_vb_2 = [Buf("vb16_0"), Buf("vb16_1")]
    Ng = [sb([128, T], BF16, "Ng") for _ in range(2)]
    NTg = [sb([128, T], BF16, "NTg") for _ in range(2)]
    Mg = [sb([128, T], BF16, "Mg") for _ in range(2)]
    B_Ng = [Buf("Ng0"), Buf("Ng1")]
    B_NTg = [Buf("NTg0"), Buf("NTg1")]
    B_Mg = [Buf("Mg0"), Buf("Mg1")]
    GT = []
    for gb in range(NGB):
        d = {}
        for nm, shp, dt in (("AR", [128, NCH, 128], BF16), ("BT", [128, T], BF16), ("KTl", [128, T], BF16),
                            ("NB", [128, NCH, 128], BF16), ("NK", [128, NCH, 128], BF16),
                            ("Minv", [128, T], BF16), ("Vtk", [128, T], BF16), ("Btk", [128, T], BF16),
                            ("Ktk", [128, T], BF16), ("WC", [128, NCH], F32), ("vfm", [128, T], F32),
                            ("gfm", [128, T], F32), ("rkS", [128, T], F32), ("Tst", [128, 64], F32),
                            ("Tb0", [128, 64], BF16), ("Tb1", [128, 64], BF16), ("Zs", [128, 64], BF16),
                            ("Ps", [128, 64], BF16), ("YT", [128, T], F32)):
            d[nm] = sb(shp, dt, nm)
            d["B_" + nm] = Buf(nm + str(gb))
        for rg in ("z", "p", "y", "t"):
            d["B_ps" + rg] = Buf("ps" + rg + str(gb))
        GT.append(d)
        memset(d["Tst"][:], 0.0, [d["B_Tst"]])
        memset(d["Tb0"][:], 0.0, [d["B_Tb0"]])
    yo = [sb([128, T], BF16, "yo") for _ in range(2)]
    B_yo = [Buf("yo0"), Buf("yo1")]
    CEXP = -math.exp(-0.5)

    def rw_prep(n, gb):
        d = GT[gb]
        vb16, Bp, Kp, B_vb = vb16_2[gb % 2], Bp_2[gb % 2], Kp_2[gb % 2], B_vb_2[gb % 2]
        t0, t1 = n * T, (n + 1) * T
        for i in range(3):
            if n == 0:
                memset(Rr[i][:, 0:1], 0.0, [B_Rr[i]])
                dma("sp", Rr[i][:, 1:T + 1], rr[i * NGB + gb, :, t0:t1], [B_rr], [B_Rr[i]])
            else:
                dma("sp", Rr[i][:, :], rr[i * NGB + gb, :, t0 - 1:t1], [B_rr], [B_Rr[i]])
            dma("sp", Ll[i][:], wl[i * NGB + gb, :, t0:t1], [B_wl], [B_Ll[i]])
        r_, k_, tmpa, sig, cs, Wt, Winv, Wp, a_, kk = tm
        Br, Bk, Bta, Bsig, Bcs, BWt, BWinv, BWp, Ba, Bkk = B_tm
        vfm = d["vfm"]
        for (src, Bs, dst, Bd, mi, mun) in ((Rr[0], B_Rr[0], r_, Br, 0, "mur"), (Rr[1], B_Rr[1], k_, Bk, 1, "muk"),
                                            (Rr[2], B_Rr[2], vfm, d["B_vfm"], 2, "muv")):
            ts(tmpa[:], src[:, 1:T + 1], omur[:, mi * NGB + gb:mi * NGB + gb + 1], ALU.mult, [Bs, B_c], [Bta])
            stt(dst[:], src[:, 0:T], V(mun, gb), tmpa[:], ALU.mult, ALU.add, [Bs, B_c, Bta], [Bd])
        cp(vb16[:], vfm[:], [d["B_vfm"]], [B_vb])
        cp(d["gfm"][:], Ll[2][:], [B_Ll[2]], [d["B_gfm"]])
        act(sig[:], Ll[0][:], AF.Sigmoid, [B_Ll[0], B_c], [Bsig], bias=V("w0", gb))
        ts(sig[:], sig[:], CEXP, ALU.mult, [Bsig], [Bsig])
        Sc.op("dve", lambda e: e.tensor_tensor_scan(cs[:], onesT[:], sig[:], 0.0, ALU.mult, ALU.add),
              reads=[Bsig, B_c], writes=[Bcs])
        if NCH > 1:
            tt(t3(tmpa)[:, 1:, :], t3(cs)[:, 1:, :], t3(cs)[:, 0:NCH - 1, 63:64].to_broadcast([128, NCH - 1, 64]),
               ALU.subtract, [Bcs], [Bta])
        cp(t3(tmpa)[:, 0, :], t3(cs)[:, 0, :], [Bcs], [Bta])
        act(Wt[:], tmpa[:], AF.Exp, [Bta], [BWt])
        act(Winv[:], tmpa[:], AF.Exp, [Bta], [BWinv], scale=-1.0)
        tt(tmpa[:], tmpa[:], sig[:], ALU.subtract, [Bta, Bsig], [Bta])
        act(Wp[:], tmpa[:], AF.Exp, [Bta], [BWp])
        cp(d["WC"][:], t3(Wt)[:, :, 63], [BWt], [d["B_WC"]])
        act(a_[:], Ll[1][:], AF.Sigmoid, [B_Ll[1], B_c], [Ba], bias=V("a0", gb))
        ts(kk[:], k_[:], V("kk", gb), ALU.mult, [Bk, B_c], [Bkk])
        tt(tmpa[:], kk[:], kk[:], ALU.mult, [Bkk], [Bta])
        mm(PS[5][:, :T], CF("blk"), tmpa[:], True, True, [Bta, B_cf], [BPS[5]])
        act(tmpa[:], PS[5][:, :T], AF.Sqrt, [BPS[5]], [Bta])
        ts(tmpa[:], tmpa[:], 1e-12, ALU.max, [Bta], [Bta])
        recip(tmpa[:], tmpa[:], [Bta], [Bta])
        tt(kk[:], kk[:], tmpa[:], ALU.mult, [Bkk, Bta], [Bkk])
        ts(tmpa[:], a_[:], -1.0, ALU.add, [Ba, B_c], [Bta], s2=V("ka", gb), op1=ALU.mult)
        stt(k_[:], tmpa[:], 1.0, k_[:], ALU.add, ALU.mult, [Bta, Bk], [Bk])
        stt(tmpa[:], r_[:], V("rk", gb), k_[:], ALU.mult, ALU.mult, [Br, Bk, B_c], [Bta])
        mm(PS[5][:, :T], CF("blk"), tmpa[:], True, True, [Bta, B_cf], [BPS[5]])
        act(d["rkS"][:], PS[5][:, :T], AF.Copy, [BPS[5]], [d["B_rkS"]])
        AR3 = d["AR"]
        stt(AR3[:, :, 0:64], t3(kk), -1.0, t3(Wp), ALU.mult, ALU.mult, [Bkk, BWp], [d["B_AR"]])
        tt(AR3[:, :, 64:128], t3(r_), t3(Wt), ALU.mult, [Br, BWt], [d["B_AR"]])
        tt(tmpa[:], kk[:], a_[:], ALU.mult, [Bkk, Ba], [Bta])
        tt(tmpa[:], tmpa[:], Winv[:], ALU.mult, [Bta, BWinv], [Bta])
        cp(d["BT"][:], tmpa[:], [Bta], [d["B_BT"]])
        tt(t3(Bp), t3(tmpa), d["WC"][:, :].unsqueeze(2).to_broadcast([128, NCH, 64]), ALU.mult,
           [Bta, d["B_WC"]], [B_vb])
        tt(tmpa[:], k_[:], Winv[:], ALU.mult, [Bk, BWinv], [Bta])
        cp(d["KTl"][:], tmpa[:], [Bta], [d["B_KTl"]])
        tt(t3(Kp), t3(tmpa), d["WC"][:, :].unsqueeze(2).to_broadcast([128, NCH, 64]), ALU.mult,
           [Bta, d["B_WC"]], [B_vb])
    def rw_prep2(n, gb):
        d = GT[gb]
        vb16, Bp, Kp, B_vb = vb16_2[gb % 2], Bp_2[gb % 2], Kp_2[gb % 2], B_vb_2[gb % 2]
        AR3 = d["AR"]
        BT3, KT3 = t3(d["BT"]), t3(d["KTl"])
        for (lt3, Bl, dst, Bd, bank0) in ((BT3, d["B_BT"], d["NB"], d["B_NB"], 0), (KT3, d["B_KTl"], d["NK"], d["B_NK"], 1)):
            for c0 in range(0, NCH, 4):
                n4 = min(4, NCH - c0)
                bank = bank0
                for cc in range(n4):
                    for hs in HS:
                        mm(PS[bank][hs, cc * 128:(cc + 1) * 128], lt3[hs, c0 + cc, :], AR3[hs, c0 + cc, :], True, True,
                           [Bl, d["B_AR"]], [BPS[bank]])
                tt(dst[:, c0:c0 + n4, :], PS[bank][:, :n4 * 128].rearrange("p (c t) -> p c t", t=128),
                   CF("mk").unsqueeze(1).to_broadcast([128, n4, 128]), ALU.mult, [BPS[bank], B_cf], [Bd])
        for cc in range(NCH):
            for hs in HS:
                mm(PS[2][hs, cc * 64:(cc + 1) * 64], AR3[hs, cc, 0:64], BT3[hs, cc, :], True, True,
                   [d["B_AR"], d["B_BT"]], [BPS[2]])
        tt(t3(NTg[0]), PS[2][:, :T].rearrange("p (c t) -> p c t", t=64),
           CF("mkl").unsqueeze(1).to_broadcast([128, NCH, 64]), ALU.mult, [BPS[2], B_cf], [B_NTg[0]])
        NB3 = d["NB"]
        tt(t3(Mg[0]), NB3[:, :, 0:64], CF("i64").unsqueeze(1).to_broadcast([128, NCH, 64]), ALU.add,
           [d["B_NB"], B_cf], [B_Mg[0]])
        curN, BcurN = NB3[:, :, 0:64], d["B_NB"]
        for lev in range(1, NLEV + 1):
            pi_, ci = (lev - 1) % 2, lev % 2
            last = lev == NLEV
            NTp = t3(NTg[pi_])
            for cc in range(NCH):
                for hs in HS:
                    if not last:
                        mm(PS[3][hs, cc * 64:(cc + 1) * 64], NTp[hs, cc, :], curN[hs, cc, :], True, True,
                           [B_NTg[pi_], BcurN], [BPS[3]])
                    mm(PS[4][hs, cc * 64:(cc + 1) * 64], curN[hs, cc, :], NTp[hs, cc, :], True, True,
                       [B_NTg[pi_], BcurN], [BPS[4]])
            if not last:
                act(Ng[ci][:], PS[3][:, :T], AF.Copy, [BPS[3]], [B_Ng[ci]])
            cp(NTg[ci][:], PS[4][:, :T], [BPS[4]], [B_NTg[ci]])
            NTc = t3(NTg[ci])
            Mp = t3(Mg[pi_])
            for cc in range(NCH):
                for hs in HS:
                    mm(PS[2][hs, cc * 64:(cc + 1) * 64], NTc[hs, cc, :], Mp[hs, cc, :], True, True,
                       [B_NTg[ci], B_Mg[pi_]], [BPS[2]])
            dstM, BdM = (d["Minv"], d["B_Minv"]) if last else (Mg[ci], B_Mg[ci])
            tt(dstM[:], PS[2][:, :T], Mg[pi_][:], ALU.add, [BPS[2], B_Mg[pi_]], [BdM])
            if not last:
                curN, BcurN = t3(Ng[ci]), B_Ng[ci]
        for ti, (src, dst, Bd) in enumerate(((vb16, d["Vtk"], d["B_Vtk"]), (Bp, d["Btk"], d["B_Btk"]),
                                             (Kp, d["Ktk"], d["B_Ktk"]))):
            tb_ = ti % 2
            pvw = PS[tb_][:].bitcast(BF16)
            s3 = t3(src)
            for cc in range(NCH):
                for hi, hs in enumerate(HS):
                    o_, w_ = clay["ident"]
                    tr(pvw[hs, cc * 64:(cc + 1) * 64], s3[hs, cc, :], cb[hs, o_ + hi * 64:o_ + hi * 64 + 64],
                       [B_vb, B_c], [BPS[tb_]])
            cp(dst[:], pvw[:, :T], [BPS[tb_]], [Bd])

    def rw_seq(n):
        for cc in range(NCH):
            gi = n * NCH + cc
            cur, nxt = "Tb%d" % (gi % 2), "Tb%d" % ((gi + 1) % 2)
            for gb in range(NGB):
                d = GT[gb]
                AR3, NB3, NK3 = d["AR"], d["NB"], d["NK"]
                Vt3, Bt3, Kt3, Mi3 = t3(d["Vtk"]), t3(d["Btk"]), t3(d["Ktk"]), t3(d["Minv"])
                zc = slice(gb * 128, gb * 128 + 64)
                pc = slice(gb * 128 + 64, gb * 128 + 128)
                for hs in HS:
                    mm(PS[6][hs, zc], AR3[hs, cc, 0:64], d[cur][hs, :], True, False, [d["B_AR"], d["B_" + cur]], [d["B_psz"]])
                    mm(PS[6][hs, zc], NK3[hs, cc, 0:64], Vt3[hs, cc, :], False, True, [d["B_NK"], d["B_Vtk"]], [d["B_psz"]])
                act(d["Zs"][:], PS[6][:, zc], AF.Copy, [d["B_psz"]], [d["B_Zs"]])
                for hs in HS:
                    mm(PS[6][hs, pc], Mi3[hs, cc, :], d["Zs"][hs, :], True, True, [d["B_Minv"], d["B_Zs"]], [d["B_psp"]])
                cp(d["Ps"][:], PS[6][:, pc], [d["B_psp"]], [d["B_Ps"]])
                for hs in HS:
                    mm(PS[7][hs, zc], d[cur][hs, :], AR3[hs, cc, 64:128], True, False, [d["B_AR"], d["B_" + cur]], [d["B_psy"]])
                    mm(PS[7][hs, zc], d["Ps"][hs, :], NB3[hs, cc, 64:128], False, False, [d["B_Ps"], d["B_NB"]], [d["B_psy"]])
                    mm(PS[7][hs, zc], Vt3[hs, cc, :], NK3[hs, cc, 64:128], False, True, [d["B_Vtk"], d["B_NK"]], [d["B_psy"]])
                act(d["YT"][:, cc * 64:(cc + 1) * 64], PS[7][:, zc], AF.Copy, [d["B_psy"]], [d["B_YT"]])
                for hs in HS:
                    mm(PS[7][hs, pc], Bt3[hs, cc, :], d["Ps"][hs, :], True, False, [d["B_Btk"], d["B_Ps"]], [d["B_pst"]])
                    mm(PS[7][hs, pc], Kt3[hs, cc, :], Vt3[hs, cc, :], False, True, [d["B_Ktk"], d["B_Vtk"]], [d["B_pst"]])
                stt(d["Tst"][:], d["Tst"][:], d["WC"][:, cc:cc + 1], PS[7][:, pc], ALU.mult, ALU.add,
                    [d["B_Tst"], d["B_WC"], d["B_pst"]], [d["B_Tst"]])
                cp(d[nxt][:], d["Tst"][:], [d["B_Tst"]], [d["B_" + nxt]])

    def rw_epi(n, gb):
        d = GT[gb]
        e0, e1, e2 = tm[0], tm[1], tm[2]
        Be0, Be1, Be2 = B_tm[0], B_tm[1], B_tm[2]
        YT = d["YT"]
        mm(PS[5][:, :T], CF("blk"), YT[:], True, True, [d["B_YT"], B_cf], [BPS[5]])
        ts(e0[:], PS[5][:, :T], 1.0 / 64, ALU.mult, [BPS[5]], [Be0])
        tt(e1[:], YT[:], e0[:], ALU.subtract, [d["B_YT"], Be0], [Be1])
        tt(e2[:], e1[:], e1[:], ALU.mult, [Be1], [Be2])
        mm(PS[5][:, :T], CF("blk"), e2[:], True, True, [Be2, B_cf], [BPS[5]])
        act(e2[:], PS[5][:, :T], AF.Sqrt, [BPS[5]], [Be2], bias=GNEPS_AP, scale=1.0 / 64)
        recip(e2[:], e2[:], [Be2], [Be2])
        tt(e1[:], e1[:], e2[:], ALU.mult, [Be1, Be2], [Be1])
        ts(e1[:], e1[:], V("lnw", gb), ALU.mult, [Be1, B_c], [Be1], s2=V("lnb", gb), op1=ALU.add)
        tt(e0[:], d["rkS"][:], d["vfm"][:], ALU.mult, [d["B_rkS"], d["B_vfm"]], [Be0])
        tt(e1[:], e1[:], e0[:], ALU.add, [Be1, Be0], [Be1])
        j = (n * NGB + gb) % 2
        tt(yo[j][:], e1[:], d["gfm"][:], ALU.mult, [Be1, d["B_gfm"]], [B_yo[j]])
        dma("pool", ybs3[n][gb, :, :], yo[j][:], [B_yo[j]], [B_ybs], key=B_yo[j])

    for n in range(NT):
        rw_prep(n, 0)
        for gb in range(NGB):
            l2 = Sc.capture(lambda: rw_prep2(n, gb))
            l1 = Sc.capture(lambda: rw_prep(n, gb + 1)) if gb + 1 < NGB else []
            Sc.replay_interleaved(l2, l1)
        rw_seq(n)
        for gb in range(NGB):
            rw_epi(n, gb)
        if "nocc" not in dbg:
            Sc.op("pool", lambda e, n=n: e.collective_compute("AllGather", ALU.bypass,
                                                              replica_groups=[[0, 1, 2, 3], [4, 5, 6, 7]],
                                                              ins=[ybs_t[n].ap().opt()], outs=[yba_t[n].ap().opt()]),
                  reads=[B_ybs], writes=[B_yba], dma=True, key=B_cc, amt=1)

    Sc.barrier()
    sbo[0] = g_mark
    LMAX = S
    KCH = min(2048, 4 * T)
    QPT = T // 128
    qiTt = sb([128, HI, 128], BF16, "qiTt")
    qTt = sb([128, HA, 128], BF16, "qTt")
    wit = sb([128, HI], F32, "wit")
    B_ql = Buf("ql")
    kit = sb([128, LMAX], BF16, "kit")
    B_kit = Buf("kit")
    acc = sb([128, LMAX], F32, "acc")
    B_acc = Buf("acc")
    rl = [sb([128, 512], F32, "rl") for _ in range(2)]
    B_rl = [Buf("rl0"), Buf("rl1")]
    kpw = sb([128, 4 * T], F32, "kpw")
    B_kpw = Buf("kpw")
    junk = sb([128, LMAX], BF16, "junk")
    B_junk = Buf("junk")
    maskT = sb([128, LMAX // 128, 128], BF16, "maskT")
    B_mT = Buf("maskT")
    Kc = [sb([128, KCH], BF16, "Kc") for _ in range(2)]
    Vc = [sb([128, KCH // 128, 128], BF16, "Vc") for _ in range(2)]
    B_Kc = [Buf("Kc0"), Buf("Kc1")]
    B_Vc = [Buf("Vc0"), Buf("Vc1")]
    PT = [sb([128, HPG * 128], BF16, "PT") for _ in range(3)]
    B_PT = [Buf("PT%d" % i) for i in range(3)]
    dn = sb([128, HPG * 128], F32, "dn")
    B_dn = Buf("dn")
    yob = [sb([128, HPG * 128], BF16, "yob") for _ in range(2)]
    B_yob = [Buf("yob0"), Buf("yob1")]
    bs = sb([128, 8], F32, "bs")
    B_bs = Buf("bs")
    lo_, hi_, wd_, mid_, cnt_, pz_ = [bs[:, i:i + 1] for i in range(6)]
    kcc = [0]
    ptc = [0]
    yoc = [0]
    NQ = HPG * 128

    qTt2 = [qTt, sb([128, HA, 128], BF16, "qTt1")]
    B_qt = [Buf("qt0"), Buf("qt1")]
    LOOK = min(6, KCH // 128)
    NPT = LOOK + 2
    while len(PT) < NPT:
        PT.append(sb([128, HPG * 128], BF16, "PT"))
        B_PT.append(Buf("PT%d" % len(B_PT)))

    def qgeom(qb):
        m_ = qb // QPT
        L = (4 * m_ + 4) * T
        return L, L // 512, L // 128, qb * 128, (qb + 1) * 128

    def idx_phase(qb):
        L, NKT, NKB, q0, q1 = qgeom(qb)
        dma("sp", qiTt[:], qiT[:, :, q0:q1].rearrange("h p t -> p h t"), [B_q], [B_ql])
        dma("sp", qTt2[qb % 2][:], qT[:, :, q0:q1].rearrange("h p t -> p h t"), [B_q], [B_qt[qb % 2]])
        dma("sp", wit[:], wi_d[q0:q1, :], [B_wi], [B_ql])
        if qb % QPT == 0:
            dma("sp", kit[:, :L], kiT[:, 0:L], [B_kiT], [B_kit])
            dma("sp", kpw[:], kpos_d[:, L - 4 * T:L].to_broadcast([128, 4 * T]), [], [B_kpw])
        for kt in range(NKT):
            ks = slice(kt * 512, (kt + 1) * 512)
            for h in range(HI):
                b_ = h % 2
                mm(PS[b_][:, :512], qiTt[:, h, :], kit[:, ks], True, True, [B_ql, B_kit], [BPS[b_]])
                act(rl[b_][:], PS[b_][:, :512], AF.Relu, [BPS[b_]], [B_rl[b_]])
                if h == 0:
                    ts(acc[:, ks], rl[b_][:], wit[:, 0:1], ALU.mult, [B_rl[b_], B_ql], [B_acc])
                else:
                    stt(acc[:, ks], rl[b_][:], wit[:, h:h + 1], acc[:, ks], ALU.mult, ALU.add,
                        [B_rl[b_], B_ql, B_acc], [B_acc])
        Sc.op("dve", lambda e, L=L: e.tensor_reduce(hi_, acc[:, :L], mybir.AxisListType.X, ALU.max,
                                                    apply_absolute_value=True),
              reads=[B_acc], writes=[B_bs])
        ts(hi_, hi_, 1.0001, ALU.mult, [B_bs], [B_bs], s2=1e-6, op1=ALU.add)
        ts(lo_, hi_, -1.0, ALU.mult, [B_bs], [B_bs])
        ts(wd_, hi_, 2.0, ALU.mult, [B_bs], [B_bs])
        for i4 in range(4 * T // 512):
            b_ = i4 % 2
            ts(rl[b_][:], kpw[:, i4 * 512:(i4 + 1) * 512], V("qpos", qb), ALU.is_gt, [B_kpw, B_c], [B_rl[b_]],
               s2=-1e30, op1=ALU.mult)
            a0_ = L - 4 * T + i4 * 512
            tt(acc[:, a0_:a0_ + 512], acc[:, a0_:a0_ + 512], rl[b_][:], ALU.add, [B_acc, B_rl[b_]], [B_acc])

    B_junkA = Buf("junkA")

    def bis_iter(qb, it):
        L = qgeom(qb)[0]
        ck = 0.5 ** (it + 1)
        stt(mid_, wd_, ck, lo_, ALU.mult, ALU.add, [B_bs], [B_bs])
        ts(junk[:, :L], acc[:, :L], mid_, ALU.is_ge, [B_acc, B_bs], [B_junk, B_bs], op1=ALU.add, accum=cnt_)
        ts(pz_, cnt_, TOPK - 0.5, ALU.is_ge, [B_bs], [B_bs], s2=ck, op1=ALU.mult)
        stt(lo_, pz_, wd_, lo_, ALU.mult, ALU.add, [B_bs], [B_bs])

    def mask_phase(qb):
        L, NKT, NKB, q0, q1 = qgeom(qb)
        ts(junk[:, :L], acc[:, :L], lo_, ALU.is_lt, [B_acc, B_bs], [B_junk, B_junkA], s2=-30000.0, op1=ALU.mult)
        pvb = PS[2][:].bitcast(BF16)
        for k0 in range(0, NKB, 8):
            n8 = min(8, NKB - k0)
            for kk_ in range(n8):
                tr(pvb[:, kk_ * 128:(kk_ + 1) * 128], junk[:, (k0 + kk_) * 128:(k0 + kk_ + 1) * 128], CB("ident"),
                   [B_junk, B_junkA, B_c], [BPS[2]])
            cp(maskT[:, k0:k0 + n8, :], pvb[:, :n8 * 128].rearrange("p (k q) -> p k q", q=128), [BPS[2]], [B_mT])

    def attn_ops(qb):
        L, NKT, NKB, q0, q1 = qgeom(qb)
        qt, Bq = qTt2[qb % 2], B_qt[qb % 2]
        ops = []
        for g in range(G):
            nchunks = (L + KCH - 1) // KCH
            steps = []
            for kc_ in range(nchunks):
                k0 = kc_ * KCH
                kn = min(KCH, L - k0)
                for kb in range(kn // 128):
                    steps.append((kc_, k0, kn, kb))
            chunk_buf = {}
            st_info = {}

            def front(si, g=g, steps=steps, chunk_buf=chunk_buf, st_info=st_info):
                kc_, k0, kn, kb = steps[si]
                if kb == 0:
                    ci = kcc[0] % 2
                    kcc[0] += 1
                    chunk_buf[kc_] = ci
                    dma("sp", Kc[ci][:, :kn], KT[g, :, k0:k0 + kn], [B_KT], [B_Kc[ci]])
                    dma("sp", Vc[ci][:, :kn // 128, :],
                        VT[k0:k0 + kn, g * 128:(g + 1) * 128].rearrange("(k p) d -> p k d", p=128), [B_VT], [B_Vc[ci]])
                ci = chunk_buf[kc_]
                kbg = k0 // 128 + kb
                sbk = (3, 4, 7)[kbg % 3]
                pj = ptc[0] % NPT
                ptc[0] += 1
                st_info[si] = (ci, pj)
                mm(PS[sbk][:, :NQ], Kc[ci][:, kb * 128:(kb + 1) * 128],
                   qt[:, g * HPG:(g + 1) * HPG, :], True, False, [B_Kc[ci], Bq], [BPS[sbk]])
                mm(PS[sbk][:, :NQ], CB("ident"), maskT[:, kbg, :].unsqueeze(1).to_broadcast([128, HPG, 128]),
                   False, True, [B_c, B_mT], [BPS[sbk]])
                act(PT[pj][:], PS[sbk][:, :NQ], AF.Exp, [BPS[sbk]], [B_PT[pj]], scale=float(128 ** -0.5))

            def back(si, g=g, steps=steps, st_info=st_info):
                kc_, k0, kn, kb = steps[si]
                ci, pj = st_info[si]
                kbg = k0 // 128 + kb
                first, last_ = (kbg == 0), (kbg == NKB - 1)
                mm(PS[5][:, :NQ], Vc[ci][:, kb, :], PT[pj][:], first, last_, [B_Vc[ci], B_PT[pj]], [BPS[5]])
                mm(PS[6][:, :NQ], CB("ones"), PT[pj][:], first, last_, [B_c, B_PT[pj]], [BPS[6]])

            def fin(g=g):
                recip(dn[:], PS[6][:, :NQ], [BPS[6]], [B_dn])
                yj = yoc[0] % 2
                yoc[0] += 1
                tt(yob[yj][:], PS[5][:, :NQ], dn[:], ALU.mult, [BPS[5], B_dn], [B_yob[yj]])
                dma("pool", yaT[g * HPG:(g + 1) * HPG, :, q0:q1].rearrange("h p t -> p h t"),
                    yob[yj][:].rearrange("p (h q) -> p h q", q=128), [B_yob[yj]], [B_ya], key=B_yob[yj])

            def step(si, front=front, back=back, nst=len(steps)):
                if si == 0:
                    for s2_ in range(min(LOOK, nst)):
                        front(s2_)
                if si + LOOK < nst:
                    front(si + LOOK)
                back(si)

            for si in range(len(steps)):
                ops.append(lambda si=si, step=step: step(si))
            ops.append(fin)
        return ops

    for qb in range(NQB + 1):
        if qb < NQB:
            idx_phase(qb)
        aops = attn_ops(qb - 1) if qb >= 1 else []
        nb = NBIS if qb < NQB else 0
        if nb and aops:
            stride = max(1, len(aops) // nb)
        bi = 0
        for ai, o in enumerate(aops):
            o()
            if nb and bi < nb and (ai + 1) % stride == 0:
                bis_iter(qb, bi)
                bi += 1
        while bi < nb:
            bis_iter(qb, bi)
            bi += 1
        if qb < NQB:
            mask_phase(qb)
    Sc.barrier()
    sbo[0] = g_mark
    r1_off = sbo[0]
    xn4 = sb([128, KC, T], BF16, "xn4")
    ya4 = sb([128, AKC, T], BF16, "ya4")
    yb4 = sb([128, YBC, T], BF16, "yb4")
    r1_end = sbo[0]
    sbo[0] = r1_off
    resid = sb([128, KC, T], F32, "resid")
    sbo[0] = max(sbo[0], r1_end)
    B_R1 = Buf("R1")
    mixt = sb([128, KC, T], BF16, "mixt")
    B_mix = Buf("mix")
    h_off = sbo[0]
    hid = sb([128, PMAX, T], BF16, "hid")
    B_hid = Buf("hid")
    h_end = sbo[0]
    sbo[0] = h_off
    ysel = [sb([128, NGB, T], BF16, "ysel") for _ in range(4)]
    sbo[0] = max(sbo[0], h_end)
    B_ysel = [Buf("ysel%d" % i) for i in range(4)]
    sq4 = [sb([128, T], F32, "sq4") for _ in range(2)]
    B_sq4 = [Buf("sq40"), Buf("sq41")]
    rst4 = sb([128, T], F32, "rst4")
    B_rst4 = Buf("rst4")
    ea = [sb([128, T], F32, "ea") for _ in range(2)]
    eb = [sb([128, T], F32, "eb") for _ in range(2)]
    B_ea = [Buf("ea0"), Buf("ea1")]
    B_eb = [Buf("eb0"), Buf("eb1")]
    ptf = sb([128, PC, T], F32, "ptf")
    ptb = sb([128, PC, T], BF16, "ptb")
    B_pt = Buf("pt")
    fo = [sb([128, T], F32, "fo") for _ in range(2)]
    B_fo = [Buf("fo0"), Buf("fo1")]
    ec = [0]

    def rms_rstd(xt_, B_x):
        for kc in range(KC):
            i = kc % 2
            act(sq4[i][:], xt_[:, kc, :], AF.Square, [B_x], [B_sq4[i]])
            mm(PS[0][:, :T], CF("ones"), sq4[i][:], kc == 0, kc == KC - 1, [B_sq4[i], B_cf], [BPS[0]])
        act(rst4[:], PS[0][:, :T], AF.Sqrt, [BPS[0]], [B_rst4], bias=EPS_AP, scale=1.0 / D)
        recip(rst4[:], rst4[:], [B_rst4], [B_rst4])

    for m_ in range(NOWN):
        tsl = slice(m_ * T, (m_ + 1) * T)
        Sc.barrier()
        for q4 in range(4):
            ksl = slice(q4 * KC // 4, (q4 + 1) * KC // 4)
            dma("sp", xn4[:, ksl, :], xno[ksl, :, tsl].rearrange("k p t -> p k t"), [B_xno], [B_R1])
        dma("sp", ya4[:], yaT[:, :, tsl].rearrange("h p t -> p h t"), [B_ya], [B_R1])
        for r_ in range(4):
            for s_ in range(4):
                dma("sp", ysel[s_][:], yba4[4 * m_ + s_][r_].rearrange("g p t -> p g t"), [B_yba], [B_ysel[s_]])
            dst = yb4[:, r_ * NGB:(r_ + 1) * NGB, :]
            ts(dst, ysel[0][:], V("sel", 0), ALU.mult, [B_ysel[0], B_c], [B_R1])
            for s_ in range(1, 4):
                stt(dst, ysel[s_][:], V("sel", s_), dst, ALU.mult, ALU.add, [B_ysel[s_], B_c, B_R1], [B_R1])
        groups, kinds = [], []
        for c_ in range(KC):
            groups += [(w_pa[c_], AKC * 128), (w_pb[c_], YBC * 128), (w_g[c_], KC * 128), (w_g[KC + c_], KC * 128)]
            kinds += [("pa", c_), ("pb", c_), ("ga", c_), ("gb", c_)]

        def consumeA(i, wt, wb):
            kind, c_ = kinds[i]
            base = 4 * (c_ % 2)
            bank = base + {"pa": 0, "pb": 1, "ga": 2, "gb": 3}[kind]
            src, nk = {"pa": (ya4, AKC), "pb": (yb4, YBC), "ga": (xn4, KC), "gb": (xn4, KC)}[kind]
            w3 = wt[:, :nk * 128].rearrange("p (k c) -> p k c", c=128)
            for kc in range(nk):
                mm(PS[bank][:, :T], w3[:, kc, :], src[:, kc, :], kc == 0, kc == nk - 1, [wb, B_R1], [BPS[bank]])
            if kind == "gb":
                j = ec[0] % 2
                ec[0] += 1
                act(ea[j][:], PS[base + 2][:, :T], AF.Sigmoid, [BPS[base + 2], B_c], [B_ea[j]], bias=V("bga", c_))
                act(eb[j][:], PS[base + 3][:, :T], AF.Sigmoid, [BPS[base + 3], B_c], [B_eb[j]], bias=V("bgb", c_))
                tt(ea[j][:], ea[j][:], PS[base + 0][:, :T], ALU.mult, [B_ea[j], BPS[base + 0]], [B_ea[j]])
                tt(eb[j][:], eb[j][:], PS[base + 1][:, :T], ALU.mult, [B_eb[j], BPS[base + 1]], [B_eb[j]])
                tt(mixt[:, c_, :], ea[j][:], eb[j][:], ALU.add, [B_ea[j], B_eb[j]], [B_mix])

        gemm_stream(groups, consumeA, next_first=(w_o[0], KC * 128))
        Sc.barrier()
        for q4 in range(4):
            ksl = slice(q4 * KC // 4, (q4 + 1) * KC // 4)
            dma("sp", resid[:, ksl, :], xoT.rearrange("(kc p) t -> p kc t", p=128)[:, ksl, tsl], [], [B_R1])

        def consumeB(i, wt, wb):
            bank = 1 + (i % 2)
            w3 = wt[:, :KC * 128].rearrange("p (k c) -> p k c", c=128)
            for kc in range(KC):
                mm(PS[bank][:, :T], w3[:, kc, :], mixt[:, kc, :], kc == 0, kc == KC - 1, [wb, B_mix], [BPS[bank]])
            tt(resid[:, i, :], resid[:, i, :], PS[bank][:, :T], ALU.add, [B_R1, BPS[bank]], [B_R1])

        gemm_stream([(w_o[c_], KC * 128) for c_ in range(KC)], consumeB, next_first=(w_f1[0], KC * 128))
        rms_rstd(resid, B_R1)
        for kc in range(KC):
            stt(mixt[:, kc, :], resid[:, kc, :], V("nffn", kc), rst4[:], ALU.mult, ALU.mult, [B_R1, B_rst4, B_c], [B_mix])
        f0 = 0
        for pi, pn in enumerate(PARTS):
            groups, kinds = [], []
            for fl in range(pn):
                groups += [(w_f1[f0 + fl], KC * 128), (w_f3[f0 + fl], KC * 128)]
                kinds += [("f1", fl), ("f3", fl)]
            for c_ in range(KC):
                groups.append((w_f2[pi * KC + c_][:, :pn * 128], pn * 128))
                kinds.append(("f2", c_))

            def consumeC(i, wt, wb, kinds=kinds, pn=pn):
                kind, idx = kinds[i]
                if kind in ("f1", "f3"):
                    bank = 4 * (idx % 2) + (0 if kind == "f1" else 1)
                    w3 = wt[:, :KC * 128].rearrange("p (k c) -> p k c", c=128)
                    for kc in range(KC):
                        mm(PS[bank][:, :T], w3[:, kc, :], mixt[:, kc, :], kc == 0, kc == KC - 1, [wb, B_mix], [BPS[bank]])
                    if kind == "f3":
                        j = ec[0] % 2
                        ec[0] += 1
                        act(ea[j][:], PS[bank - 1][:, :T], AF.Silu, [BPS[bank - 1]], [B_ea[j]])
                        tt(hid[:, idx, :], ea[j][:], PS[bank][:, :T], ALU.mult, [B_ea[j], BPS[bank]], [B_hid])
                else:
                    bank = 2 + (idx % 2)
                    w3 = wt[:, :pn * 128].rearrange("p (k c) -> p k c", c=128)
                    for kc in range(pn):
                        mm(PS[bank][:, :T], w3[:, kc, :], hid[:, kc, :], kc == 0, kc == pn - 1, [wb, B_hid], [BPS[bank]])
                    tt(resid[:, idx, :], resid[:, idx, :], PS[bank][:, :T], ALU.add, [B_R1, BPS[bank]], [B_R1])

            f0 += pn
            gemm_stream(groups, consumeC, next_first=((w_f1[f0], KC * 128) if pi + 1 < len(PARTS) else (w_pg[0], KC * 128)))
        rms_rstd(resid, B_R1)
        for kc in range(KC):
            tt(mixt[:, kc, :], resid[:, kc, :], rst4[:], ALU.mult, [B_R1, B_rst4], [B_mix])
        dma("sp", ptf[:], pT.rearrange("(k p) t -> p k t", p=128)[:, :, tsl], [], [B_pt])
        cp(ptb[:], ptf[:], [B_pt], [B_pt])
        groups, kinds = [], []
        for c_ in range(KC):
            groups += [(w_pg[c_], KC * 128), (w_pl[c_], PC * 128)]
            kinds += [("pg", c_), ("pl", c_)]

        def consumeD(i, wt, wb, kinds=kinds):
            kind, c_ = kinds[i]
            bank = 4 * (c_ % 2) + (0 if kind == "pg" else 1)
            src, Bs, nk = (mixt, B_mix, KC) if kind == "pg" else (ptb, B_pt, PC)
            w3 = wt[:, :nk * 128].rearrange("p (k c) -> p k c", c=128)
            for kc in range(nk):
                mm(PS[bank][:, :T], w3[:, kc, :], src[:, kc, :], kc == 0, kc == nk - 1, [wb, Bs], [BPS[bank]])
            if kind == "pl":
                j = ec[0] % 2
                ec[0] += 1
                act(ea[j][:], PS[bank - 1][:, :T], AF.Sigmoid, [BPS[bank - 1]], [B_ea[j]])
                tt(ea[j][:], ea[j][:], PS[bank][:, :T], ALU.mult, [B_ea[j], BPS[bank]], [B_ea[j]])
                tt(resid[:, c_, :], resid[:, c_, :], ea[j][:], ALU.add, [B_R1, B_ea[j]], [B_R1])

        gemm_stream(groups, consumeD, next_first=((w_pa[0], AKC * 128) if m_ + 1 < NOWN else None))
        rms_rstd(resid, B_R1)
        for kc in range(KC):
            j = kc % 2
            stt(fo[j][:], resid[:, kc, :], V("nfin", kc), rst4[:], ALU.mult, ALU.mult, [B_R1, B_rst4, B_c], [B_fo[j]])
            dma("pool", outT[kc * 128:(kc + 1) * 128, tsl], fo[j][:], [B_fo[j]], [B_out], key=B_fo[j])
    Sc.emit(nc, es)
    es.close()
    return nc, dbg_out


def own_tokens(c, j):
    T = c["T"]
    return np.concatenate([np.arange((4 * m + j) * T, (4 * m + j + 1) * T) for m in range(c["NOWN"])])


def wlay(W, cw=128):
    K, N = W.shape
    kc, ng = K // 128, N // cw
    return np.ascontiguousarray(W.reshape(kc, 128, ng, cw).transpose(2, 1, 0, 3)).reshape(ng, 128, kc * cw)


def pvec(v):
    return np.ascontiguousarray(np.asarray(v, np.float32).reshape(-1, 128).T)


def prep_inputs(cfg, inp):
    c = derive(cfg)
    S, D, T, KC = c["S"], c["D"], c["T"], c["KC"]
    HA, G, HI, DB, DBc, NGB, DA = c["HA"], c["G"], c["HI"], c["DB"], c["DBc"], c["NGB"], c["DA"]
    PARTS, PMAX, FC = c["PARTS"], c["PMAX"], c["FC"]
    f = lambda a: np.asarray(a, np.float32)
    w_in = f(inp["w_in"])[0]
    o = 0
    sl = {}
    for n, sz in (("q", DA), ("k", G * 128), ("v", G * 128), ("qi", HI * 128), ("ki", 128), ("wi", HI),
                  ("rb", DB), ("kb", DB), ("vb", DB)):
        sl[n] = (o, o + sz)
        o += sz
    W = lambda n: w_in[:, sl[n][0]:sl[n][1]]
    shared = {}
    shared["w_kv"] = np.concatenate([wlay(W("k")), wlay(W("v"))], 0)
    shared["w_ki"] = wlay(W("ki"))
    shared["w_l1"] = wlay(f(inp["w1"])[0], c["LW"])
    shared["w_a1"] = wlay(f(inp["a1"])[0], c["LA"])
    shared["w_g1"] = wlay(f(inp["g1"])[0])
    shared["w_q"] = wlay(W("q"))
    shared["w_qi"] = wlay(W("qi"))
    shared["w_wi"] = wlay(W("wi"), HI)
    shared["w_pa"] = wlay(f(inp["w_pa"])[0])
    shared["w_pb"] = wlay(f(inp["w_pb"])[0])
    shared["w_g"] = wlay(f(inp["w_gate"])[0])
    shared["w_o"] = wlay(f(inp["w_o"])[0])
    shared["w_f1"] = wlay(f(inp["w_ffn1"])[0])
    shared["w_f3"] = wlay(f(inp["w_ffn3"])[0])
    w2 = f(inp["w_ffn2"])[0]
    wf2 = np.zeros((4 * KC, 128, PMAX * 128), np.float32)
    r0 = 0
    for pi, pn in enumerate(PARTS):
        blk = wlay(w2[r0 * 128:(r0 + pn) * 128, :])
        wf2[pi * KC:(pi + 1) * KC, :, :pn * 128] = blk
        r0 += pn
    shared["w_f2"] = wf2
    shared["w_pg"] = wlay(f(inp["w_ple_gate"])[0])
    shared["w_pl"] = wlay(f(inp["w_ple"])[0])
    shared["kpos"] = np.arange(S, dtype=np.float32)[None, :]
    clay, NCF = cf_layout()
    cfa = np.zeros((128, NCF), np.float32)
    r = np.arange(128)
    cfa[:, clay["ident"][0]:clay["ident"][0] + 128] = np.eye(128)
    cfa[:, clay["ones"][0]:clay["ones"][0] + 128] = 1.0
    cfa[:, clay["blk"][0]:clay["blk"][0] + 128] = (r[:, None] // 64 == r[None, :] // 64)
    cfa[:, clay["prot"][0]:clay["prot"][0] + 128] = (r[:, None] == (r[None, :] + 64) % 128)
    col = r[None, :]
    row = r[:, None] % 64
    cfa[:, clay["mk"][0]:clay["mk"][0] + 128] = np.where(col < 64, col > row, (col - 64) >= row)
    cfa[:, clay["mkl"][0]:clay["mkl"][0] + 64] = (row > np.arange(64)[None, :])
    cfa[:, clay["caus"][0]:clay["caus"][0] + 128] = np.where(r[None, :] <= r[:, None], 0.0, -1e30)
    cfa[:, clay["i64"][0]:clay["i64"][0] + 64] = (row == np.arange(64)[None, :])
    shared["cf"] = cfa
    vlay, NV = vec_layout(c)
    x = f(inp["x"])
    p = f(inp["p"])[0]
    posn = np.asarray(inp["positions"]).astype(np.int32)
    invf = (np.float32(10000.0) ** (-np.arange(0, 128, 2, dtype=np.float32) / np.float32(128))).astype(np.float32)
    xTb = [np.ascontiguousarray(x[b].T) for b in range(2)]
    maps = []
    for core in range(8):
        b, j = core // 4, core % 4
        own = own_tokens(c, j)
        ch = slice(j * DBc, (j + 1) * DBc)
        m = dict(shared)
        m["xT"] = xTb[b]
        m["xoT"] = np.ascontiguousarray(x[b][own].T)
        m["pos"] = np.ascontiguousarray(posn[b][None, :])
        m["poso"] = np.ascontiguousarray(posn[b][own][None, :])
        m["pT"] = np.ascontiguousarray(p[b][own].T)
        m["w_rkv"] = np.concatenate([wlay(W("rb")[:, ch]), wlay(W("kb")[:, ch]), wlay(W("vb")[:, ch])], 0)
        m["w2"] = np.ascontiguousarray(f(inp["w2"])[0][:, ch])
        m["a2"] = np.ascontiguousarray(f(inp["a2"])[0][:, ch])
        g2 = f(inp["g2"])[0][:, ch]
        m["g2"] = np.ascontiguousarray(g2.reshape(-1, 128, DBc).transpose(1, 0, 2)).reshape(128, -1)
        v = np.zeros((128, NV), np.float32)

        def put(name, arr):
            o_, w_ = vlay[name]
            v[:, o_:o_ + w_] = arr
        put("nmix", pvec(f(inp["norm_mix"])[0]))
        mw = f(inp["mu_wag"])[0]
        put("muw", pvec(mw[0])); put("mua", pvec(mw[1])); put("mug", pvec(mw[2]))
        put("nffn", pvec(f(inp["norm_ffn"])[0])); put("nfin", pvec(f(inp["norm_final"])))
        bg = f(inp["b_gate"])[0]
        put("bga", pvec(bg[:D])); put("bgb", pvec(bg[D:]))
        mr = f(inp["mu_rkv"])[0]
        put("mur", pvec(mr[0][ch])); put("muk", pvec(mr[1][ch])); put("muv", pvec(mr[2][ch]))
        put("w0", pvec(f(inp["w0"])[0][ch])); put("a0", pvec(f(inp["a0"])[0][ch]))
        put("kk", pvec(f(inp["k_k"])[0][ch])); put("ka", pvec(f(inp["k_a"])[0][ch]))
        put("rk", pvec(f(inp["r_k"])[0].reshape(-1)[ch]))
        put("lnw", pvec(f(inp["ln_w"])[0][ch])); put("lnb", pvec(f(inp["ln_b"])[0][ch]))
        put("invf", invf[np.arange(128) % 64][:, None])
        put("sgn", np.where(np.arange(128) < 64, -1.0, 1.0)[:, None])
        selv = np.zeros((128, 4), np.float32)
        selv[:, j] = 1.0
        put("sel", selv)
        put("qpos", own.astype(np.float32).reshape(-1, 128).T)
        m["vecs"] = v
        maps.append(m)
    return maps


_CACHE = {}


def run_cfg(cfg, inp, dbg=()):
    key = (tuple(sorted(cfg.items())), tuple(dbg))
    if key not in _CACHE:
        _CACHE[key] = build(cfg, dbg)
    nc, dbg_out = _CACHE[key]
    maps = prep_inputs(cfg, inp)
    res = run_bass_kernel_spmd(nc, maps, core_ids=list(range(8)))
    c = derive(cfg)
    out = np.zeros((2, c["S"], c["D"]), np.float32)
    for core in range(8):
        b, j = core // 4, core % 4
        out[b][own_tokens(c, j)] = np.asarray(res.results[core]["outT"]).T
    return out, res


def kernel(**inputs):
    out, _ = run_cfg(FULL, inputs)
    return out
```
